# Optimizing a Trainium2 kernel written in Bass

```python
import math
import jax
import jax.numpy as jnp
from jax import lax
import numpy as np

D_MODEL = 1024
BATCH = 4
SEQ = 8192
DEPTH = 1
DEC_BATCH = 32
DEC_SEQ = 8
PAST_LEN = 16384
PAGE_SIZE = 128

A_GROUPS = ((128, 1), (512, 4), (2048, 16))
A_HEADS_PER_GROUP = 4
A_HEADS = A_HEADS_PER_GROUP * len(A_GROUPS)
A_HEAD_DIM = 128
A_WIDTH = A_HEADS * A_HEAD_DIM
A_OUT = A_HEADS_PER_GROUP * A_HEAD_DIM
REL_BUCKETS = 32
REL_MAX_DIST = 2048
B_HEADS = 4
B_DK = 128
B_DV = 256
B_QK = B_HEADS * B_DK
B_V = B_HEADS * B_DV
B_GATE_RANK = 16
B_GATE_NORM = 16.0
B_CHUNK = 64
MEM_LEN = 256
M_HEADS = 4
M_HEAD_DIM = D_MODEL // M_HEADS
M_WIDTH = M_HEADS * M_HEAD_DIM
D_FF = -(-8 * D_MODEL // (3 * 256)) * 256
IN_WIDTHS = (A_WIDTH, A_WIDTH, A_WIDTH, B_QK, B_QK, B_V, B_V, B_GATE_RANK, D_MODEL, D_MODEL)
EPS = 1e-6
NEG_INF = -1e30

kernel_name = 'hybrid_dilated_gla_decoder_step'


def rmsnorm(x, g):
    x32 = x.astype(jnp.float32)
    y = x32 * lax.rsqrt(jnp.mean(x32 * x32, axis=-1, keepdims=True) + EPS)
    return (y * g.astype(jnp.float32)).astype(x.dtype)


def t5_bucket(n):
    max_exact = REL_BUCKETS // 2
    nf = jnp.maximum(n, 1).astype(jnp.float32)
    large = max_exact + (jnp.log(nf / max_exact) / math.log(REL_MAX_DIST / max_exact)
                         * (REL_BUCKETS - max_exact)).astype(jnp.int32)
    return jnp.where(n < max_exact, n, jnp.minimum(large, REL_BUCKETS - 1))


def softmax_with_lse(s):
    mx = jnp.max(s, axis=-1, keepdims=True)
    p = jnp.exp(s - mx)
    den = jnp.sum(p, axis=-1, keepdims=True)
    return p / den, (mx + jnp.log(den))[..., 0]


def split_columns(t):
    parts, start = [], 0
    for w in IN_WIDTHS:
        parts.append(t[..., start:start + w])
        start += w
    return parts


def dilated_prompt(q, k, v, bias_tab, window, dil):
    bsz, seq, heads, dh = q.shape
    nk = window // dil
    span = dil * nk
    s_pad = -(-seq // span) * span
    m = s_pad // dil
    nb = m // nk

    def to_blocks(t):
        t = jnp.pad(t, ((0, 0), (0, s_pad - seq), (0, 0), (0, 0)))
        t = t.reshape(bsz, m, dil, heads, dh).transpose(0, 2, 1, 3, 4)
        return t.reshape(bsz, dil, nb, nk, heads, dh)

    def band(t):
        prev = jnp.pad(t[:, :, :-1], ((0, 0), (0, 0), (1, 0), (0, 0), (0, 0), (0, 0)))
        return jnp.concatenate([prev, t], axis=3)

    qb = to_blocks(q)
    kband = band(to_blocks(k))
    vband = band(to_blocks(v))
    i = jnp.arange(nk)[:, None]
    j = jnp.arange(2 * nk)[None, :]
    dsub = nk + i - j
    in_band = (dsub >= 0) & (dsub <= nk)
    blk = jnp.arange(nb)[:, None, None]
    valid = in_band[None] & (blk * nk + j[None] - nk >= 0)
    bias = bias_tab[t5_bucket(jnp.maximum(dsub, 0) * dil)].transpose(2, 0, 1)
    s = jnp.einsum('brnqhd,brnkhd->brnhqk', qb, kband).astype(jnp.float32) * (dh ** -0.5)
    s = s + bias.astype(jnp.float32)[None, None, None]
    s = jnp.where(valid[None, None, :, None], s, NEG_INF)
    p, lse = softmax_with_lse(s)
    o = jnp.einsum('brnhqk,brnkhd->brnqhd', p.astype(v.dtype), vband)
    o = o.reshape(bsz, dil, m, heads, dh).transpose(0, 2, 1, 3, 4).reshape(bsz, s_pad, heads, dh)[:, :seq]
    lse = lse.transpose(0, 1, 2, 4, 3).reshape(bsz, dil, m, heads).transpose(0, 2, 1, 3)
    lse = lse.reshape(bsz, s_pad, heads)[:, :seq]
    return o, lse


def dilated_sample(q, k, v, buf_k, buf_v, bias_tab, window, dil):
    bsz, seq, heads, dh = q.shape
    wb = buf_k.shape[1]
    nk = window // dil
    kc = jnp.concatenate([buf_k, k], axis=1)
    vc = jnp.concatenate([buf_v, v], axis=1)
    steps = jnp.arange(nk + 1)
    idx = wb + jnp.arange(seq)[:, None] - steps[None, :] * dil
    valid = idx >= 0
    idx = jnp.maximum(idx, 0)
    kg = kc[:, idx]
    vg = vc[:, idx]
    bias = bias_tab[t5_bucket(steps * dil)].T
    s = jnp.einsum('bqhd,bqkhd->bhqk', q, kg).astype(jnp.float32) * (dh ** -0.5)
    s = s + bias.astype(jnp.float32)[None, :, None, :]
    s = jnp.where(valid[None, None], s, NEG_INF)
    p, lse = softmax_with_lse(s)
    o = jnp.einsum('bhqk,bqkhd->bqhd', p.astype(v.dtype), vg)
    return o, lse.transpose(0, 2, 1)


def combine_dilations(outs, lses):
    w = jax.nn.softmax(jnp.stack(lses, axis=0), axis=0)
    o = jnp.sum(w[..., None] * jnp.stack(outs, axis=0).astype(jnp.float32), axis=0)
    return o.astype(outs[0].dtype)


def gla_chunked(q, k, v, logf, s0, chunk):
    bsz, seq, heads, dk = q.shape
    dv = v.shape[-1]
    nc = seq // chunk

    def prep(t):
        t = t.astype(jnp.float32).reshape(bsz, nc, chunk, heads, t.shape[-1])
        return t.transpose(1, 0, 3, 2, 4)

    qc = prep(q) * (dk ** -0.5)
    kc, vc, fc = prep(k), prep(v), prep(logf)
    mask = jnp.tril(jnp.ones((chunk, chunk), dtype=bool))
    mid = chunk // 2

    def step(state, inp):
        qx, kx, vx, fx = inp
        b = jnp.cumsum(fx, axis=2)
        bm = b[:, :, mid:mid + 1]
        blast = b[:, :, -1:]
        o_inter = jnp.einsum('bhcd,bhde->bhce', qx * jnp.exp(b), state)
        a = jnp.einsum('bhid,bhjd->bhij', qx * jnp.exp(b - bm), kx * jnp.exp(bm - b))
        a = jnp.where(mask, a, 0.0)
        o = o_inter + jnp.einsum('bhij,bhje->bhie', a, vx)
        state = jnp.exp(blast[:, :, 0])[..., None] * state + jnp.einsum('bhcd,bhce->bhde', kx * jnp.exp(blast - b), vx)
        return state, o

    s_fin, o = lax.scan(step, s0.astype(jnp.float32), (qc, kc, vc, fc))
    o = o.transpose(1, 0, 3, 2, 4).reshape(bsz, seq, heads, dv)
    return o, s_fin


def memory_kv(mem, g, w_mk, w_mv):
    bsz, mlen, _ = mem.shape
    m = rmsnorm(mem, g)
    k = (m @ w_mk).reshape(bsz, mlen, M_HEADS, M_HEAD_DIM)
    v = (m @ w_mv).reshape(bsz, mlen, M_HEADS, M_HEAD_DIM)
    return jnp.stack([k, v], axis=2)


def cross_attend(h, mem_kv, w_mq, w_mo):
    bsz, seq, _ = h.shape
    q = (h @ w_mq).reshape(bsz, seq, M_HEADS, M_HEAD_DIM)
    s = jnp.einsum('bqhd,bmhd->bhqm', q, mem_kv[:, :, 0]).astype(jnp.float32) * (M_HEAD_DIM ** -0.5)
    p = jax.nn.softmax(s, axis=-1).astype(h.dtype)
    o = jnp.einsum('bhqm,bmhd->bqhd', p, mem_kv[:, :, 1])
    return o.reshape(bsz, seq, M_WIDTH) @ w_mo


def trunk_layer(x, mem_kv, gla_s0, gla_chunk, win_bufs, rel_bias, lw):
    (n_mix_pre, n_mix_post, w_in, w_f2, b_f2, gla_norm, w_proj_a, w_proj_b, w_out,
     n_mem_pre, n_mem_post, w_mq, w_mo, n_ffn_pre, n_ffn_post, w_gate, w_up, w_down) = lw
    bsz, seq, _ = x.shape
    h = rmsnorm(x, n_mix_pre)
    qa, ka, va, qb, kb, vb, gb, fb, gate_a, gate_b = split_columns(h @ w_in)
    qa = qa.reshape(bsz, seq, A_HEADS, A_HEAD_DIM)
    ka = ka.reshape(bsz, seq, A_HEADS, A_HEAD_DIM)
    va = va.reshape(bsz, seq, A_HEADS, A_HEAD_DIM)
    outs, lses, win_new = [], [], []
    for gi, (window, dil) in enumerate(A_GROUPS):
        hs = slice(gi * A_HEADS_PER_GROUP, (gi + 1) * A_HEADS_PER_GROUP)
        tab = rel_bias[:, hs]
        qg, kg, vg = qa[:, :, hs], ka[:, :, hs], va[:, :, hs]
        if win_bufs is None:
            o, l = dilated_prompt(qg, kg, vg, tab, window, dil)
            keep = min(window, seq)
            win_new.append(jnp.stack([kg[:, seq - keep:], vg[:, seq - keep:]], axis=2))
        else:
            buf = win_bufs[gi]
            o, l = dilated_sample(qg, kg, vg, buf[:, :, 0], buf[:, :, 1], tab, window, dil)
            win_new.append(jnp.stack([kg, vg], axis=2))
        outs.append(o)
        lses.append(l)
    ya = combine_dilations(outs, lses).reshape(bsz, seq, A_OUT) @ w_proj_a
    qb = qb.reshape(bsz, seq, B_HEADS, B_DK)
    kb = kb.reshape(bsz, seq, B_HEADS, B_DK)
    vb = vb.reshape(bsz, seq, B_HEADS, B_DV)
    logf = jax.nn.log_sigmoid((fb @ w_f2 + b_f2).astype(jnp.float32)) / B_GATE_NORM
    logf = logf.reshape(bsz, seq, B_HEADS, B_DK)
    ob, s_new = gla_chunked(qb, kb, vb, logf, gla_s0, gla_chunk)
    ob = rmsnorm(ob.astype(x.dtype), gla_norm) * jax.nn.silu(gb).reshape(bsz, seq, B_HEADS, B_DV)
    yb = ob.reshape(bsz, seq, B_V) @ w_proj_b
    mix = (jax.nn.sigmoid(gate_a) * ya + jax.nn.sigmoid(gate_b) * yb) @ w_out
    x = x + rmsnorm(mix, n_mix_post)
    h = rmsnorm(x, n_mem_pre)
    x = x + rmsnorm(cross_attend(h, mem_kv, w_mq, w_mo), n_mem_post)
    h = rmsnorm(x, n_ffn_pre)
    ffn = (jax.nn.silu(h @ w_gate) * (h @ w_up)) @ w_down
    x = x + rmsnorm(ffn, n_ffn_post)
    return x, win_new, s_new.astype(x.dtype)


def setup_inputs(seed: int = 0) -> dict:
    key = jax.random.key(seed)
    ks = iter(jax.random.split(key, 32))
    f32 = jnp.float32

    def nrm(shape, scale=1.0):
        return jax.random.normal(next(ks), shape, f32) * scale

    def gain(shape):
        return 1.0 + 0.05 * nrm(shape)

    in_total = sum(IN_WIDTHS)
    wb = [min(w, PAST_LEN) for w, _ in A_GROUPS]
    return {
        'x_prompt': nrm((BATCH, SEQ, D_MODEL)),
        'x_sample': nrm((DEC_BATCH, DEC_SEQ, D_MODEL)),
        'cache_win1_kv': nrm((DEPTH, DEC_BATCH, wb[0], 2, A_HEADS_PER_GROUP, A_HEAD_DIM)),
        'cache_win2_kv': nrm((DEPTH, DEC_BATCH, wb[1], 2, A_HEADS_PER_GROUP, A_HEAD_DIM)),
        'cache_win3_kv': nrm((DEPTH, DEC_BATCH, wb[2], 2, A_HEADS_PER_GROUP, A_HEAD_DIM)),
        'state_gla': nrm((DEPTH, DEC_BATCH, B_HEADS, B_DK, B_DV), 0.5),
        'cache_mem_kv': nrm((DEPTH, DEC_BATCH, MEM_LEN, 2, M_HEADS, M_HEAD_DIM)),
        'mem_prompt': nrm((BATCH, MEM_LEN, D_MODEL)),
        'rel_bias': nrm((REL_BUCKETS, A_HEADS), 0.5),
        'norm_mix_pre': gain((DEPTH, D_MODEL)),
        'norm_mix_post': gain((DEPTH, D_MODEL)),
        'w_in': nrm((DEPTH, D_MODEL, in_total), D_MODEL ** -0.5),
        'w_f2': nrm((DEPTH, B_GATE_RANK, B_QK), B_GATE_RANK ** -0.5),
        'b_f2': nrm((DEPTH, B_QK), 0.1),
        'gla_norm': gain((DEPTH, B_DV)),
        'w_proj_a': nrm((DEPTH, A_OUT, D_MODEL), A_OUT ** -0.5),
        'w_proj_b': nrm((DEPTH, B_V, D_MODEL), B_V ** -0.5),
        'w_out': nrm((DEPTH, D_MODEL, D_MODEL), D_MODEL ** -0.5),
        'norm_memtok': gain((DEPTH, D_MODEL)),
        'w_mk': nrm((DEPTH, D_MODEL, M_WIDTH), D_MODEL ** -0.5),
        'w_mv': nrm((DEPTH, D_MODEL, M_WIDTH), D_MODEL ** -0.5),
        'norm_mem_pre': gain((DEPTH, D_MODEL)),
        'norm_mem_post': gain((DEPTH, D_MODEL)),
        'w_mq': nrm((DEPTH, D_MODEL, M_WIDTH), D_MODEL ** -0.5),
        'w_mo': nrm((DEPTH, M_WIDTH, D_MODEL), M_WIDTH ** -0.5),
        'norm_ffn_pre': gain((DEPTH, D_MODEL)),
        'norm_ffn_post': gain((DEPTH, D_MODEL)),
        'w_ffn_gate': nrm((DEPTH, D_MODEL, D_FF), D_MODEL ** -0.5),
        'w_ffn_up': nrm((DEPTH, D_MODEL, D_FF), D_MODEL ** -0.5),
        'w_ffn_down': nrm((DEPTH, D_FF, D_MODEL), D_FF ** -0.5),
    }


def reference(x_prompt, x_sample, cache_win1_kv, cache_win2_kv, cache_win3_kv, state_gla, cache_mem_kv,
              mem_prompt, rel_bias, norm_mix_pre, norm_mix_post, w_in, w_f2, b_f2, gla_norm, w_proj_a,
              w_proj_b, w_out, norm_memtok, w_mk, w_mv, norm_mem_pre, norm_mem_post, w_mq, w_mo,
              norm_ffn_pre, norm_ffn_post, w_ffn_gate, w_ffn_up, w_ffn_down):
    yp, ys = x_prompt, x_sample
    p_w1, p_w2, p_w3, p_gla, p_mem = [], [], [], [], []
    s_w1, s_w2, s_w3, s_gla = [], [], [], []
    for l in range(DEPTH):
        lw = (norm_mix_pre[l], norm_mix_post[l], w_in[l], w_f2[l], b_f2[l], gla_norm[l], w_proj_a[l],
              w_proj_b[l], w_out[l], norm_mem_pre[l], norm_mem_post[l], w_mq[l], w_mo[l],
              norm_ffn_pre[l], norm_ffn_post[l], w_ffn_gate[l], w_ffn_up[l], w_ffn_down[l])
        mem_kv_p = memory_kv(mem_prompt, norm_memtok[l], w_mk[l], w_mv[l])
        s0 = jnp.zeros((x_prompt.shape[0], B_HEADS, B_DK, B_DV), jnp.float32)
        yp, win_p, gla_p = trunk_layer(yp, mem_kv_p, s0, min(B_CHUNK, x_prompt.shape[1]), None, rel_bias, lw)
        bufs = (cache_win1_kv[l], cache_win2_kv[l], cache_win3_kv[l])
        ys, win_s, gla_s = trunk_layer(ys, cache_mem_kv[l], state_gla[l], x_sample.shape[1], bufs, rel_bias, lw)
        p_w1.append(win_p[0])
        p_w2.append(win_p[1])
        p_w3.append(win_p[2])
        p_gla.append(gla_p)
        p_mem.append(mem_kv_p)
        s_w1.append(win_s[0])
        s_w2.append(win_s[1])
        s_w3.append(win_s[2])
        s_gla.append(gla_s)
    return (yp, ys, jnp.stack(p_w1), jnp.stack(p_w2), jnp.stack(p_w3), jnp.stack(p_gla), jnp.stack(p_mem),
            jnp.stack(s_w1), jnp.stack(s_w2), jnp.stack(s_w3), jnp.stack(s_gla))
```

```python
import math
import types
import contextlib
import numpy as np
import concourse.bass as bass
import concourse.mybir as mybir
from concourse.bass_utils import run_bass_kernel_spmd

F32 = mybir.dt.float32
BF16 = mybir.dt.bfloat16
AF = mybir.ActivationFunctionType
ALU = mybir.AluOpType


class Buf:
    __slots__ = ("name", "last_w", "readers")

    def __init__(self, name=""):
        self.name = name
        self.last_w = None
        self.readers = []


class T:
    def __init__(self, t, name=""):
        self.t = t
        self.b = Buf(name)

    def __getitem__(self, k):
        return self.t[k]


class View:
    def __init__(self, parent, fn):
        self.b = parent.b
        self.fn = fn

    def __getitem__(self, k):
        return self.fn()[k]


class Op:
    __slots__ = ("eng", "fn", "is_dma", "deps", "marked", "ordinal", "dsem", "dval")

    def __init__(self, eng, fn, is_dma):
        self.eng = eng
        self.fn = fn
        self.is_dma = is_dma
        self.deps = []
        self.marked = False
        self.ordinal = 0
        self.dsem = None
        self.dval = 0


ENGS = ("pe", "act", "dve", "pool", "sp")
NDSEM = 8


def _b(x):
    return x.b if hasattr(x, "b") else x


def _freeze(fn):
    if fn.__closure__ is None:
        return fn
    cells = []
    for c in fn.__closure__:
        try:
            cells.append(types.CellType(c.cell_contents))
        except ValueError:
            cells.append(c)
    return types.FunctionType(fn.__code__, fn.__globals__, fn.__name__, fn.__defaults__, tuple(cells))


class Prog:
    def __init__(self, nc):
        self.nc = nc
        self.ops = {e: [] for e in ENGS}
        self.pending = {e: None for e in ENGS}

    def op(self, eng, fn, reads=(), writes=(), dma=False):
        o = Op(eng, _freeze(fn), dma)
        reads = [_b(x) for x in reads]
        writes = [_b(x) for x in writes]
        deps = {}
        for b in reads:
            w = b.last_w
            if w is not None:
                deps[id(w)] = (w, True)
        for b in writes:
            w = b.last_w
            if w is not None and id(w) not in deps:
                deps[id(w)] = (w, False)
            for r in b.readers:
                if id(r) not in deps:
                    deps[id(r)] = (r, False)
        for (d, raw) in deps.values():
            if d is o:
                continue
            if d.is_dma:
                o.deps.append(d)
                d.marked = True
                continue
            if d.eng == eng and not dma:
                if eng == "pe" or not raw:
                    continue
            o.deps.append(d)
            d.marked = True
        pb = self.pending[eng]
        if pb is not None:
            if eng != "sp":
                o.deps.append(pb)
            self.pending[eng] = None
        for b in writes:
            b.last_w = o
            b.readers = []
        for b in reads:
            if b.last_w is not o:
                if not dma:
                    b.readers = [r for r in b.readers if r.is_dma or r.eng != eng]
                b.readers.append(o)
        self.ops[eng].append(o)
        return o

    def barrier(self):
        deps = []
        for e in ENGS:
            lst = self.ops[e]
            for o in reversed(lst):
                if not o.is_dma:
                    deps.append(o)
                    break
            cnt = 0
            for o in reversed(lst):
                if o.is_dma:
                    deps.append(o)
                    cnt += 1
                    if cnt >= NDSEM:
                        break
        bo = Op("sp", lambda e: e.nop(), False)
        for d in deps:
            if d.eng == "sp" and not d.is_dma:
                continue
            bo.deps.append(d)
            d.marked = True
        bo.marked = True
        self.ops["sp"].append(bo)
        self.pending = {e: bo for e in ENGS}

    def emit(self):
        nc = self.nc
        with contextlib.ExitStack() as st:
            csem = {e: st.enter_context(nc.semaphore("c_" + e)) for e in ENGS}
            dsems = {e: [st.enter_context(nc.semaphore("d_%s%d" % (e, i))) for i in range(NDSEM)]
                     for e in ENGS if e != "pe"}
            for e in ENGS:
                k = 0
                nd = 0
                for o in self.ops[e]:
                    if o.is_dma:
                        o.dsem = (e, nd % NDSEM)
                        o.dval = 16 * (nd // NDSEM + 1)
                        nd += 1
                    elif o.marked:
                        k += 1
                        o.ordinal = k
            block = st.enter_context(nc.Block())
            engobj = {"pe": "tensor", "act": "scalar", "dve": "vector", "pool": "gpsimd", "sp": "sync"}

            def run(e, eng):
                seen_c = {x: 0 for x in ENGS}
                seen_d = {}
                for o in self.ops[e]:
                    for d in o.deps:
                        if d.is_dma:
                            key = d.dsem
                            if seen_d.get(key, 0) < d.dval:
                                eng.wait_ge(dsems[key[0]][key[1]], d.dval)
                                seen_d[key] = d.dval
                        else:
                            if seen_c[d.eng] < d.ordinal:
                                eng.wait_ge(csem[d.eng], d.ordinal)
                                seen_c[d.eng] = d.ordinal
                    if o.is_dma:
                        key = o.dsem
                        prev = o.dval - 16
                        if prev > 0 and seen_d.get(key, 0) < prev:
                            eng.wait_ge(dsems[key[0]][key[1]], prev)
                            seen_d[key] = prev
                        ins = o.fn(eng)
                        ins.then_inc(dsems[key[0]][key[1]], 16)
                    else:
                        ins = o.fn(eng)
                        if o.marked:
                            ins.then_inc(csem[e], 1)
                last = {}
                for o in self.ops[e]:
                    if o.is_dma:
                        last[o.dsem] = o.dval
                for key, v in last.items():
                    if seen_d.get(key, 0) < v:
                        eng.wait_ge(dsems[key[0]][key[1]], v)

            for e in ENGS:
                if not self.ops[e]:
                    continue

                def mk(e):
                    def f(eng):
                        run(e, eng)
                    return f
                getattr(block, engobj[e])(mk(e))


D = 1024
NPRE = 4096
NOWN = 4096
NS = 32
NTOK = NPRE + NOWN + NS
SOFF = NPRE + NOWN
INW = 9744
QA, KA, VA, QB, KB, VB, GB, FB, GA, GBT = 0, 1536, 3072, 4608, 5120, 5632, 6656, 7680, 7696, 8720
DFF = 2816
GROUPS = ((128, 1), (512, 4), (2048, 16))
EPS = 1e-6
SC_A = 128 ** -0.5
SC_B = 128 ** -0.5
SC_M = 256 ** -0.5


def t5_bucket_np(n):
    n = np.asarray(n, dtype=np.int64)
    nf = np.maximum(n, 1).astype(np.float32)
    large = 16 + (np.log(nf / np.float32(16)) / np.float32(math.log(2048 / 16)) * np.float32(16)).astype(np.int32)
    return np.where(n < 16, n, np.minimum(large, 31))


def onehot_tables():
    oh = np.zeros((3, 32, 129), np.float32)
    for g, (w, dil) in enumerate(GROUPS):
        bk = t5_bucket_np(np.arange(129) * dil)
        for d in range(129):
            oh[g, bk[d], d] = 1.0
    return oh


WEIGHTS = [("w_in", D, INW), ("w_proj_a", 512, D), ("w_proj_b", D, D), ("w_out", D, D), ("w_mk", D, D),
           ("w_mv", D, D), ("w_mq", D, D), ("w_mo", D, D), ("w_ffn_gate", D, DFF), ("w_ffn_up", D, DFF),
           ("w_ffn_down", DFF, D)]
NORMS = ["norm_mix_pre", "norm_mix_post", "norm_memtok", "norm_mem_pre", "norm_mem_post", "norm_ffn_pre",
         "norm_ffn_post"]


STOP = None
DBG_GROUPS = (0, 1, 2)
DBG_SAMPLE = True
DBG_PROMPT = True
DBG_SKIP12 = False
DBG_A = 9
PIPE_DEPTH = 2
PIPE_STAGGER = 6


def run_pipeline(gens, depth, stagger):
    it = iter(gens)
    active = []
    done = False
    while True:
        if not done and len(active) < depth and (not active or active[-1][1] >= stagger):
            try:
                active.append([next(it), 0])
            except StopIteration:
                done = True
        if not active:
            if done:
                break
            continue
        for a in list(active):
            try:
                next(a[0])
                a[1] += 1
            except StopIteration:
                active.remove(a)
DBG_W = True


def build():
    nc = bass.Bass("TRN2", target_bir_lowering=False)
    P = Prog(nc)

    def din(name, shape, dt=F32):
        return nc.dram_tensor(name, shape, dt, kind="ExternalInput").ap()

    def dout(name, shape):
        return nc.dram_tensor(name, shape, F32, kind="ExternalOutput").ap()

    def dscr(name, shape, dt):
        return nc.dram_tensor(name, shape, dt, kind="Internal").ap()

    xp = din("xp", [NPRE + NOWN, D])
    xs = din("xs", [NS, D])
    flag = din("flag", [128, 1])
    oh_d = din("oh", [3, 32, 129])
    cw = [din("cw1", [4, 128, 2, 512]), din("cw2", [4, 512, 2, 512]), din("cw3", [4, 2048, 2, 512])]
    sgla = din("sgla", [4, 4, 128, 256])
    cmem = din("cmem", [4, 256, 2, 1024])
    mem = din("mem", [256, D])
    relb = din("rel_bias", [32, 12])
    wf2 = din("w_f2", [16, 512])
    bf2 = din("b_f2", [1, 512])
    glan = din("gla_norm", [1, 256])
    nrm = {n: din(n, [1, D]) for n in NORMS}
    wsrc = {n: din(n, [k, m]) for (n, k, m) in WEIGHTS}

    y_o = dout("y", [NOWN, D])
    ys_o = dout("ys", [NS, D])
    wp_o = [dout("w1p", [128, 2, 512]), dout("w2p", [512, 2, 512]), dout("w3p", [2048, 2, 512])]
    glap_o = dout("glap", [4, 128, 256])
    memkv_o = dout("memkv", [256, 2, 1024])
    ws_o = [dout("w1s", [NS, 2, 512]), dout("w2s", [NS, 2, 512]), dout("w3s", [NS, 2, 512])]
    glas_o = dout("glas", [4, 4, 128, 256])

    hTd = dscr("hTd", [D, NTOK], BF16)
    accd = dscr("accd", [3, NOWN + NS, 516], F32)
    mixpd = dscr("mixpd", [D, NOWN + NS], F32)
    x2d = dscr("x2d", [NOWN + NS, D], F32)
    vvd = dscr("vvd", [3, 4, 384], F32)
    obTd = dscr("obTd", [D, NOWN + NS], BF16)
    sgad = dscr("sgad", [D, NOWN + NS], BF16)

    hTv = hTd.rearrange("(c p) t -> p c t", p=128)
    mixpv = mixpd.rearrange("(c p) t -> p c t", p=128)
    obTv = obTd.rearrange("(c p) t -> p c t", p=128)
    sgav = sgad.rearrange("(c p) t -> p c t", p=128)

    def wview(name):
        return wsrc[name].rearrange("(c p) n -> p c n", p=128)

    with contextlib.ExitStack() as G:
        uid = [0]

        def sb(st, name, shape, dt):
            uid[0] += 1
            name = "%s_%d" % (name, uid[0])
            return T(st.enter_context(nc.sbuf_tensor(name, shape, dt)), name)

        psF = [T(G.enter_context(nc.psum_tensor("psF%d" % i, [128, 512], F32)), "psF%d" % i) for i in range(6)]
        psB = [T(G.enter_context(nc.psum_tensor("psB%d" % i, [128, 1024], BF16)), "psB%d" % i) for i in range(2)]
        cnt = {"f": 0, "b": 0, "s": 0}

        def getS():
            return getF()

        def getF():
            cnt["f"] += 1
            return psF[cnt["f"] % 6]

        def getB():
            cnt["b"] += 1
            return psB[cnt["b"] % 2]

        ident_f = sb(G, "ident_f", [128, 128], F32)
        ident_b = sb(G, "ident_b", [128, 128], BF16)
        Jf = sb(G, "Jf", [128, 128], F32)
        Uf = sb(G, "Uf", [128, 128], F32)
        triU = sb(G, "triU", [128, 128], F32)
        triL = sb(G, "triL", [128, 128], F32)
        ones_f = sb(G, "ones_f", [128, 128], F32)
        ones_b = sb(G, "ones_b", [128, 2], BF16)
        m16 = sb(G, "m16", [128, 2], F32)
        flag_sb = sb(G, "flag_sb", [128, 1], F32)
        GE = contextlib.ExitStack()
        gT = {n: sb(G, "gT_" + n, [128, 8], F32) for n in ("norm_mix_pre", "norm_memtok", "norm_mem_pre", "norm_ffn_pre")}

        Eg = [sb(GE, "E%d" % g, [128, 4, 256], BF16) for g in range(3)]
        Egf = [sb(GE, "Ef%d" % g, [128, 4, 256], BF16) for g in range(3)]

        def pool(fn, r=(), w=()):
            return P.op("pool", fn, r, w)

        def dve(fn, r=(), w=()):
            return P.op("dve", fn, r, w)

        def act(fn, r=(), w=()):
            return P.op("act", fn, r, w)

        def pe(fn, r=(), w=()):
            return P.op("pe", fn, r, w)

        def dma(fn, r=(), w=(), q="sp"):
            return P.op(q, fn, r, w, dma=True)

        def mm(out, lhsT, rhs, r, w, start=True, stop=True):
            return pe(lambda e: e.matmul(out, lhsT=lhsT, rhs=rhs, start=start, stop=stop), r, w)

        with contextlib.ExitStack() as S:
            pool(lambda e: e.memset(ones_f[:], 1.0), w=[ones_f])
            pool(lambda e: e.memset(ones_b[:], 1.0), w=[ones_b])
            pool(lambda e: e.memset(m16[:], -1.0 / 16), w=[m16])
            pool(lambda e: e.memset(ident_f[:], 1.0), w=[ident_f])
            pool(lambda e: e.affine_select(out=ident_f[:], in_=ident_f[:], pattern=[[-1, 128]], compare_op=ALU.is_equal,
                                           fill=0.0, base=0, channel_multiplier=1), r=[ident_f], w=[ident_f])
            pool(lambda e: e.memset(Jf[:], 1.0), w=[Jf])
            pool(lambda e: e.affine_select(out=Jf[:], in_=Jf[:], pattern=[[1, 128]], compare_op=ALU.is_equal,
                                           fill=0.0, base=-127, channel_multiplier=1), r=[Jf], w=[Jf])
            pool(lambda e: e.memset(Uf[:], 1.0), w=[Uf])
            pool(lambda e: e.affine_select(out=Uf[:], in_=Uf[:], pattern=[[1, 128]], compare_op=ALU.is_ge,
                                           fill=0.0, base=0, channel_multiplier=-1), r=[Uf], w=[Uf])
            dve(lambda e: e.tensor_copy(out=ident_b[:], in_=ident_f[:]), r=[ident_f], w=[ident_b])
            dve(lambda e: e.tensor_scalar(out=triU[:], in0=Uf[:], scalar1=-1.0 / 16, scalar2=None, op0=ALU.mult), r=[Uf], w=[triU])
            dve(lambda e: e.tensor_scalar(out=triL[:], in0=Uf[:], scalar1=1.0 / 16, scalar2=-1.0 / 16, op0=ALU.mult, op1=ALU.add),
                r=[Uf], w=[triL])
            dma(lambda e: e.dma_start(out=flag_sb[:], in_=flag), w=[flag_sb])
            for n in gT:
                dma(lambda e, n=n: e.dma_start(out=gT[n][:], in_=nrm[n].rearrange("o (c p) -> p (o c)", p=128),
                                               allow_slow_non_contiguous=True), w=[gT[n]])
            tab = sb(S, "tab", [32, 12], F32)
            ohs = sb(S, "ohs", [32, 3, 129], F32)
            vv = sb(S, "vv", [4, 3, 384], F32)
            hk = sb(S, "hk", [128, 4, 256], F32)
            dma(lambda e: e.dma_start(out=tab[:], in_=relb), w=[tab])
            dma(lambda e: e.dma_start(out=ohs[:], in_=oh_d.rearrange("g b d -> b g d")), w=[ohs])
            pool(lambda e: e.memset(vv[:], 0.0), w=[vv])
            for g in range(3):
                p = getF()
                mm(p[0:4, 0:129], tab[:, 4 * g:4 * g + 4], ohs[:, g, :], [tab, ohs], [p])
                act(lambda e, g=g, p=p: e.activation(out=vv[:, g, 127:256], in_=p[0:4, 0:129], func=AF.Exp), r=[p, vv], w=[vv])
            dma(lambda e: e.dma_start(out=vvd.rearrange("g h m -> h g m"), in_=vv[:]), r=[vv], w=[])
            P.barrier()
            for g in range(3):
                src = bass.AP(vvd.tensor, g * 4 * 384, [[1, 128], [384, 4], [1, 256]])
                dma(lambda e, src=src: e.dma_start(out=hk[:], in_=src), w=[hk])
                for h in range(4):
                    p = getF()
                    mm(p[:, 0:256], Jf[:], hk[:, h, :], [Jf, hk], [p])
                    act(lambda e, g=g, h=h, p=p: e.copy(out=Eg[g][:, h, :], in_=p[:, 0:256]), r=[p], w=[Eg[g]])
                dve(lambda e, g=g: e.tensor_copy(out=Egf[g][:, :, 0:128], in_=Eg[g][:, :, 0:128]), r=[Eg[g]], w=[Egf[g]])
                dve(lambda e, g=g: e.tensor_scalar(out=Egf[g][:, :, 128:256], in0=Eg[g][:, :, 128:256], scalar1=flag_sb[:, 0:1],
                                                   scalar2=None, op0=ALU.mult), r=[Eg[g], flag_sb], w=[Egf[g]])
            P.barrier()

        if STOP == 0:
            GE.close()
            P.emit()
            return nc
        def rms_stats(st_tag, src_ap, nt, ss, rstd, junk, src_bufs, width):
            act(lambda e: e.activation(out=junk[0:nt, 0:width], in_=src_ap, func=AF.Square, accum_out=ss[0:nt, 0:1]),
                r=list(src_bufs) + [junk], w=[junk, ss])
            act(lambda e: e.activation(out=rstd[0:nt, :], in_=ss[0:nt, :], func=AF.Ln, scale=1.0 / width, bias=EPS), r=[ss], w=[rstd])
            act(lambda e: e.activation(out=rstd[0:nt, :], in_=rstd[0:nt, :], func=AF.Exp, scale=-0.5), r=[rstd], w=[rstd])

        def norm_part1(x_t, nt, xn, ss, rstd, junk):
            rms_stats("", x_t[0:nt, :], nt, ss, rstd, junk, [x_t], D)
            dve(lambda e: e.tensor_scalar(out=xn[0:nt, :], in0=x_t[0:nt, :], scalar1=rstd[0:nt, 0:1], scalar2=None, op0=ALU.mult),
                r=[x_t, rstd], w=[xn])

        def norm_part2(nt, gname, xn, dstT, c0):
            p = getB()
            for c in range(8):
                pe(lambda e, c=c, p=p: e.transpose(out=p[:, c * nt:(c + 1) * nt], in_=xn[0:nt, c * 128:(c + 1) * 128],
                                                    identity=ident_b[0:nt, 0:nt]), r=[xn, ident_b], w=[p])
            dve(lambda e, p=p: e.tensor_tensor(out=dstT[:, :, c0:c0 + nt], in0=p[:, 0:8 * nt].rearrange("p (c t) -> p c t", c=8),
                                               in1=gT[gname][:, :].unsqueeze(2).broadcast_to([128, 8, nt]), op=ALU.mult),
                r=[p, gT[gname]], w=[dstT])

        def norm_transpose(x_t, nt, gname, xn, ss, rstd, junk, dstT, c0):
            norm_part1(x_t, nt, xn, ss, rstd, junk)
            norm_part2(nt, gname, xn, dstT, c0)

        if STOP == 1:
            GE.close()
            P.emit()
            return nc
        with contextlib.ExitStack() as S:
            Wq = sb(S, "g_Wq", [128, 8, 512], BF16)
            Wk = sb(S, "g_Wk", [128, 8, 512], BF16)
            Wv = sb(S, "g_Wv", [128, 8, 1024], BF16)
            Wg = sb(S, "g_Wg", [128, 8, 1024], BF16)
            Wfb = sb(S, "g_Wfb", [128, 8, 16], BF16)
            wv_in = wview("w_in")
            for (wt, c0, n) in ((Wfb, FB, 16), (Wv, VB, 1024), (Wk, KB, 512), (Wq, QB, 512), (Wg, GB, 1024)):
                dma(lambda e, wt=wt, c0=c0, n=n: e.dma_start(out=wt[:], in_=wv_in[:, :, c0:c0 + n]), w=[wt], q="pool")
            wf2_sb = sb(S, "g_wf2", [17, 512], F32)
            gn_sb = sb(S, "g_gn", [128, 256], F32)
            dma(lambda e: e.dma_start(out=wf2_sb[0:16, :], in_=wf2), w=[wf2_sb])
            dma(lambda e: e.dma_start(out=wf2_sb[16:17, :], in_=bf2), w=[wf2_sb])
            dma(lambda e: e.dma_start(out=gn_sb[:], in_=glan[0:1, :].broadcast_to([128, 256])), w=[gn_sb])
            Sst = sb(S, "g_S", [128, 4, 256], F32)
            NSB = 5
            Sbfs = [sb(S, "g_Sbf%d" % i, [128, 4, 256], BF16) for i in range(NSB)]
            hsm = sb(S, "g_hsm", [128, 8, NS], BF16)
            junk = sb(S, "g_junk", [128, 256], BF16)
            junkD = sb(S, "g_junkD", [128, D], BF16)

            class Set2:
                pass
            NGS = 4
            gsets = []
            for si in range(NGS):
                B_ = Set2()
                B_.xt = sb(S, "g_xt", [128, D], F32)
                B_.xn = sb(S, "g_xn", [128, D], BF16)
                B_.ss = sb(S, "g_ss", [128, 1], F32)
                B_.rstd = sb(S, "g_rstd", [128, 1], F32)
                B_.hT = sb(S, "g_hTs", [128, 8, 128], BF16)
                B_.fbT = sb(S, "g_fbT", [32, 128], F32)
                pool(lambda e, t_=B_.fbT: e.memset(t_[:], 1.0), w=[B_.fbT])
                B_.spl = sb(S, "g_sp", [128, 512], F32)
                B_.Ed = sb(S, "g_Ed", [128, 512], F32)
                B_.t1 = B_.Ed
                B_.kd = sb(S, "g_kd", [128, 512], BF16)
                B_.vsb = sb(S, "g_v", [128, 1024], BF16)
                B_.eb = sb(S, "g_eb", [128, 4], F32)
                B_.Epos = sb(S, "g_Ep", [128, 512], F32)
                B_.Eneg = sb(S, "g_En", [128, 512], F32)
                B_.qeT = sb(S, "g_qeT", [128, 4, 128], BF16)
                B_.kbT = sb(S, "g_kbT", [128, 4, 128], BF16)
                B_.aTm = sb(S, "g_aTm", [128, 4, 128], BF16)
                B_.osb = B_.xt
                B_.ssq = sb(S, "g_ssq", [128, 4], F32)
                B_.rsq = sb(S, "g_rsq", [128, 4], F32)
                B_.sg = sb(S, "g_sg", [128, 1024], F32)
                B_.ob = B_.xn
                B_.obT = sb(S, "g_obT", [128, 8, 128], BF16)
                gsets.append(B_)

            def gla_tile(B_, hT, c0, nt, full, mix_col, sidx, xrow=None):
                fbT, t1, spl, Ed, kd, vsb, eb = B_.fbT, B_.t1, B_.spl, B_.Ed, B_.kd, B_.vsb, B_.eb
                Epos, Eneg, qeT, kbT, aTm, osb, ssq, rsq, sg, ob, obT = (B_.Epos, B_.Eneg, B_.qeT, B_.kbT, B_.aTm, B_.osb, B_.ssq,
                                                                         B_.rsq, B_.sg, B_.ob, B_.obT)
                Sb_in = Sbfs[sidx]
                Sb_out = Sbfs[(sidx + 1) % NSB]
                if xrow is not None:
                    dma(lambda e: e.dma_start(out=B_.xt[0:nt, :], in_=xp[xrow:xrow + nt, :]), w=[B_.xt])
                    yield
                    norm_part1(B_.xt, nt, B_.xn, B_.ss, B_.rstd, junkD)
                    yield
                    norm_part2(nt, "norm_mix_pre", B_.xn, hT, c0)
                    dma(lambda e: e.dma_start(out=hTv[:, :, xrow:xrow + nt], in_=hT[:, :, c0:c0 + nt]), r=[hT])
                    yield
                p = getS()
                for k in range(8):
                    mm(p[0:16, 0:nt], Wfb[:, k, :], hT[:, k, c0:c0 + nt], [Wfb, hT], [p], start=(k == 0), stop=(k == 7))
                act(lambda e, p=p: e.copy(out=fbT[0:16, 0:nt], in_=p[0:16, 0:nt]), r=[p], w=[fbT])
                for half in range(2):
                    p = getF()
                    for k in range(8):
                        mm(p[0:nt, :], hT[:, k, c0:c0 + nt], Wv[:, k, half * 512:(half + 1) * 512], [hT, Wv], [p],
                           start=(k == 0), stop=(k == 7))
                    act(lambda e, p=p, half=half: e.copy(out=vsb[0:nt, half * 512:(half + 1) * 512], in_=p[0:nt, :]), r=[p], w=[vsb])
                yield
                p = getF()
                mm(p[0:nt, :], fbT[0:17, 0:nt], wf2_sb[:], [fbT, wf2_sb], [p])
                act(lambda e, p=p: e.activation(out=t1[0:nt, :], in_=p[0:nt, :], func=AF.Exp, scale=-1.0), r=[p], w=[t1])
                act(lambda e: e.activation(out=spl[0:nt, :], in_=t1[0:nt, :], func=AF.Ln, bias=1.0), r=[t1], w=[spl])
                yield
                pkt = getF()
                for k in range(8):
                    mm(pkt[0:nt, :], hT[:, k, c0:c0 + nt], Wk[:, k, :], [hT, Wk], [pkt], start=(k == 0), stop=(k == 7))
                if full:
                    pq = getF()
                    pk = getF()
                    for h in range(4):
                        for k in range(8):
                            mm(pq[:, h * nt:(h + 1) * nt], Wq[:, k, h * 128:(h + 1) * 128], hT[:, k, c0:c0 + nt], [Wq, hT], [pq],
                               start=(k == 0), stop=(k == 7))
                        for k in range(8):
                            mm(pk[:, h * nt:(h + 1) * nt], Wk[:, k, h * 128:(h + 1) * 128], hT[:, k, c0:c0 + nt], [Wk, hT], [pk],
                               start=(k == 0), stop=(k == 7))
                p = getF()
                mm(p[0:nt, :], triL[0:nt, 0:nt], spl[0:nt, :], [triL, spl], [p])
                act(lambda e, p=p: e.activation(out=Ed[0:nt, :], in_=p[0:nt, :], func=AF.Exp), r=[p], w=[Ed])
                dve(lambda e, pkt=pkt: e.tensor_tensor(out=kd[0:nt, :], in0=pkt[0:nt, :], in1=Ed[0:nt, :], op=ALU.mult), r=[pkt, Ed], w=[kd])
                if not full:
                    p = getS()
                    for h in range(4):
                        mm(p[:, h:h + 1], spl[0:nt, h * 128:(h + 1) * 128], m16[0:nt, 0:1], [spl, m16], [p])
                    act(lambda e, p=p: e.activation(out=eb[:], in_=p[:, 0:4], func=AF.Exp), r=[p], w=[eb])
                if full:
                    pb_ = getF()
                    for h in range(4):
                        mm(pb_[:, h * nt:(h + 1) * nt], spl[0:nt, h * 128:(h + 1) * 128], triU[0:nt, 0:nt], [spl, triU], [pb_])
                    act(lambda e, pb_=pb_: e.activation(out=eb[:], in_=pb_[:, 0:4 * nt].rearrange("p (h t) -> p h t", h=4)[:, :, nt - 1],
                                                        func=AF.Exp), r=[pb_], w=[eb])
                    act(lambda e, pb_=pb_: e.activation(out=Epos[:, 0:4 * nt], in_=pb_[:, 0:4 * nt], func=AF.Exp), r=[pb_], w=[Epos])
                    act(lambda e, pb_=pb_: e.activation(out=Eneg[:, 0:4 * nt], in_=pb_[:, 0:4 * nt], func=AF.Exp, scale=-1.0), r=[pb_], w=[Eneg])
                    dve(lambda e, pq=pq: e.scalar_tensor_tensor(out=qeT[:, :, 0:nt], in0=pq[:, 0:4 * nt].rearrange("p (h t) -> p h t", h=4),
                                                                scalar=SC_B, in1=Epos[:, 0:4 * nt].rearrange("p (h t) -> p h t", h=4),
                                                                op0=ALU.mult, op1=ALU.mult), r=[pq, Epos], w=[qeT])
                    dve(lambda e, pk=pk: e.tensor_tensor(out=kbT[:, :, 0:nt], in0=pk[:, 0:4 * nt].rearrange("p (h t) -> p h t", h=4),
                                                         in1=Eneg[:, 0:4 * nt].rearrange("p (h t) -> p h t", h=4), op=ALU.mult),
                        r=[pk, Eneg], w=[kbT])
                yield
                for hp in range(2):
                    p = getF()
                    for hh in range(2):
                        h = hp * 2 + hh
                        mm(p[:, hh * 256:(hh + 1) * 256], kd[0:nt, h * 128:(h + 1) * 128], vsb[0:nt, h * 256:(h + 1) * 256], [kd, vsb], [p])
                    for hh in range(2):
                        h = hp * 2 + hh
                        dve(lambda e, p=p, h=h, hh=hh: e.scalar_tensor_tensor(out=Sst[:, h, :], in0=Sst[:, h, :], scalar=eb[:, h:h + 1],
                                                                              in1=p[:, hh * 256:(hh + 1) * 256], op0=ALU.mult, op1=ALU.add),
                            r=[Sst, eb, p], w=[Sst])
                pool(lambda e: e.tensor_copy(out=Sb_out[:], in_=Sst[:]), r=[Sst], w=[Sb_out])
                if not full:
                    return
                yield
                pa = getF()
                for h in range(4):
                    mm(pa[0:nt, h * nt:(h + 1) * nt], kbT[:, h, 0:nt], qeT[:, h, 0:nt], [kbT, qeT], [pa])
                dve(lambda e, pa=pa: e.tensor_tensor(out=aTm[0:nt, :, 0:nt], in0=pa[0:nt, 0:4 * nt].rearrange("p (h t) -> p h t", h=4),
                                                     in1=Uf[0:nt, 0:nt].unsqueeze(1).broadcast_to([nt, 4, nt]), op=ALU.mult),
                    r=[pa, Uf], w=[aTm])
                for half in range(2):
                    p = getF()
                    for k in range(8):
                        mm(p[0:nt, :], hT[:, k, c0:c0 + nt], Wg[:, k, half * 512:(half + 1) * 512], [hT, Wg], [p],
                           start=(k == 0), stop=(k == 7))
                    act(lambda e, p=p, half=half: e.activation(out=sg[0:nt, half * 512:(half + 1) * 512], in_=p[0:nt, :], func=AF.Silu),
                        r=[p], w=[sg])
                pool(lambda e: e.tensor_tensor(out=sg[0:nt, :].rearrange("p (h e) -> p h e", h=4),
                                               in0=sg[0:nt, :].rearrange("p (h e) -> p h e", h=4),
                                               in1=gn_sb[0:nt, :].unsqueeze(1).broadcast_to([nt, 4, 256]), op=ALU.mult),
                     r=[sg, gn_sb], w=[sg])
                yield
                for hp in range(2):
                    p = getF()
                    for hh in range(2):
                        h = hp * 2 + hh
                        mm(p[0:nt, hh * 256:(hh + 1) * 256], qeT[:, h, 0:nt], Sb_in[:, h, :], [qeT, Sb_in], [p], start=True, stop=False)
                        mm(p[0:nt, hh * 256:(hh + 1) * 256], aTm[0:nt, h, 0:nt], vsb[0:nt, h * 256:(h + 1) * 256], [aTm, vsb], [p],
                           start=False, stop=True)
                    act(lambda e, p=p, hp=hp: e.copy(out=osb[0:nt, hp * 512:(hp + 1) * 512], in_=p[0:nt, :]), r=[p], w=[osb])
                for h in range(4):
                    act(lambda e, h=h: e.activation(out=junk[0:nt, :], in_=osb[0:nt, h * 256:(h + 1) * 256], func=AF.Square,
                                                    accum_out=ssq[0:nt, h:h + 1]), r=[osb, junk], w=[junk, ssq])
                act(lambda e: e.activation(out=rsq[0:nt, :], in_=ssq[0:nt, :], func=AF.Ln, scale=1.0 / 256, bias=EPS), r=[ssq], w=[rsq])
                act(lambda e: e.activation(out=rsq[0:nt, :], in_=rsq[0:nt, :], func=AF.Exp, scale=-0.5), r=[rsq], w=[rsq])
                yield
                for h in range(4):
                    dve(lambda e, h=h: e.scalar_tensor_tensor(out=ob[0:nt, h * 256:(h + 1) * 256], in0=osb[0:nt, h * 256:(h + 1) * 256],
                                                              scalar=rsq[0:nt, h:h + 1], in1=sg[0:nt, h * 256:(h + 1) * 256],
                                                              op0=ALU.mult, op1=ALU.mult), r=[osb, rsq, sg], w=[ob])
                yield
                pt = getB()
                for c in range(8):
                    pe(lambda e, c=c, pt=pt: e.transpose(out=pt[:, c * nt:(c + 1) * nt], in_=ob[0:nt, c * 128:(c + 1) * 128],
                                                          identity=ident_b[0:nt, 0:nt]), r=[ob, ident_b], w=[pt])
                act(lambda e, pt=pt: e.copy(out=obT[:, :, 0:nt], in_=pt[:, 0:8 * nt].rearrange("p (c t) -> p c t", c=8)), r=[pt], w=[obT])
                mc = mix_col
                dma(lambda e: e.dma_start(out=obTv[:, :, mc:mc + nt], in_=obT[:, :, 0:nt]), r=[obT])

            pool(lambda e: e.memset(Sst[:], 0.0), w=[Sst])
            pool(lambda e: e.memset(Sbfs[0][:], 0.0), w=[Sbfs[0]])
            NG = (NPRE + NOWN) // 512

            def gla_gens():
                tcount = 0
                for grp in range(0 if DBG_SKIP12 else NG):
                    for s_ in range(4):
                        tok = grp * 512 + s_ * 128
                        full = tok >= NPRE
                        B_ = gsets[tcount % NGS]
                        yield gla_tile(B_, B_.hT, 0, 128, full, tok - NPRE, tcount % NSB, tok)
                        tcount += 1
            run_pipeline(gla_gens(), 4, 3)
            nt_total = 0 if DBG_SKIP12 else NG * 4
            dma(lambda e: e.dma_start(out=glap_o.rearrange("h d e -> d h e"), in_=Sst[:]), r=[Sst])
            hT = hsm
            B0 = gsets[0]
            dma(lambda e: e.dma_start(out=B0.xt[0:NS, :], in_=xs), w=[B0.xt])
            norm_transpose(B0.xt, NS, "norm_mix_pre", B0.xn, B0.ss, B0.rstd, junkD, hsm, 0)
            dma(lambda e: e.dma_start(out=hTv[:, :, SOFF:SOFF + NS], in_=hsm[:]), r=[hsm])
            for b in range(0 if DBG_SKIP12 else 4):
                dma(lambda e, b=b: e.dma_start(out=Sst[:], in_=sgla[b].rearrange("h d e -> d h e")), w=[Sst])
                act(lambda e: e.copy(out=Sbfs[0][:], in_=Sst[:]), r=[Sst], w=[Sbfs[0]])
                for _ in gla_tile(gsets[b % 2], hT, b * 8, 8, True, NOWN + b * 8, 0):
                    pass
                dma(lambda e, b=b: e.dma_start(out=glas_o[b].rearrange("h d e -> d h e"), in_=Sst[:]), r=[Sst])
            P.barrier()

        with contextlib.ExitStack() as S:
            Wga = sb(S, "h_Wga", [128, 8, 1024], BF16)
            Wgb = sb(S, "h_Wgb", [128, 8, 1024], BF16)
            Wpb = sb(S, "h_Wpb", [128, 8, 1024], BF16)
            dma(lambda e: e.dma_start(out=Wgb[:], in_=wview("w_in")[:, :, GBT:GBT + 1024]), w=[Wgb], q="pool")
            dma(lambda e: e.dma_start(out=Wpb[:], in_=wview("w_proj_b")), w=[Wpb], q="pool")
            dma(lambda e: e.dma_start(out=Wga[:], in_=wview("w_in")[:, :, GA:GA + 1024]), w=[Wga], q="pool")
            hTg = [sb(S, "h_hT%d" % i, [128, 8, 512], BF16) for i in range(2)]
            obg = [sb(S, "h_ob%d" % i, [128, 8, 512], BF16) for i in range(2)]
            sgt2 = [sb(S, "h_sg%d" % i, [128, 512], F32) for i in range(2)]
            partg = [sb(S, "h_part%d" % i, [128, 8, 512], F32) for i in range(2)]
            sgag = [sb(S, "h_sga%d" % i, [128, 8, 512], BF16) for i in range(2)]

            def gate_group(gi):
                smp = gi == NOWN // 512
                ntok = NS if smp else 512
                hcol = SOFF if smp else NPRE + gi * 512
                mcol = gi * 512
                hT, obt, pg_, sa_ = hTg[gi % 2], obg[gi % 2], partg[gi % 2], sgag[gi % 2]
                dma(lambda e: e.dma_start(out=hT[:, :, 0:ntok], in_=hTv[:, :, hcol:hcol + ntok]), w=[hT])
                dma(lambda e: e.dma_start(out=obt[:, :, 0:ntok], in_=obTv[:, :, mcol:mcol + ntok]), w=[obt])
                yield
                for c in range(8):
                    if c == 4:
                        yield
                    p = getF()
                    for k in range(8):
                        mm(p[:, 0:ntok], Wgb[:, k, c * 128:(c + 1) * 128], hT[:, k, 0:ntok], [Wgb, hT], [p], start=(k == 0), stop=(k == 7))
                    sg_ = sgt2[c % 2]
                    act(lambda e, p=p, sg_=sg_: e.activation(out=sg_[:, 0:ntok], in_=p[:, 0:ntok], func=AF.Sigmoid), r=[p], w=[sg_])
                    p = getF()
                    for k in range(8):
                        mm(p[:, 0:ntok], Wpb[:, k, c * 128:(c + 1) * 128], obt[:, k, 0:ntok], [Wpb, obt], [p], start=(k == 0), stop=(k == 7))
                    dve(lambda e, p=p, sg_=sg_, c=c: e.tensor_tensor(out=pg_[:, c, 0:ntok], in0=p[:, 0:ntok], in1=sg_[:, 0:ntok], op=ALU.mult),
                        r=[p, sg_], w=[pg_])
                    p = getF()
                    for k in range(8):
                        mm(p[:, 0:ntok], Wga[:, k, c * 128:(c + 1) * 128], hT[:, k, 0:ntok], [Wga, hT], [p], start=(k == 0), stop=(k == 7))
                    act(lambda e, p=p, c=c: e.activation(out=sa_[:, c, 0:ntok], in_=p[:, 0:ntok], func=AF.Sigmoid), r=[p], w=[sa_])
                dma(lambda e: e.dma_start(out=mixpv[:, :, mcol:mcol + ntok], in_=pg_[:, :, 0:ntok]), r=[pg_], q="pool")
                dma(lambda e: e.dma_start(out=sgav[:, :, mcol:mcol + ntok], in_=sa_[:, :, 0:ntok]), r=[sa_], q="pool")

            run_pipeline([gate_group(gi) for gi in range(0 if DBG_SKIP12 else NOWN // 512 + 1)], 2, 1)
            P.barrier()

        if STOP == 2:
            GE.close()
            P.emit()
            return nc
        def attn_group(g, W, dil, S):
            Wq = sb(S, "a_Wq", [128, 8, 512], BF16)
            Wk = sb(S, "a_Wk", [128, 8, 512], BF16)
            Wv = sb(S, "a_Wv", [128, 8, 512], BF16)
            wv_in = wview("w_in")
            for (wt, c0) in ((Wk, KA + 512 * g), (Wv, VA + 512 * g), (Wq, QA + 512 * g)):
                dma(lambda e, wt=wt, c0=c0: e.dma_start(out=wt[:], in_=wv_in[:, :, c0:c0 + 512]), w=[wt], q="pool")
            hT = sb(S, "a_hT", [128, 8, 2048], BF16)
            hTb = [Buf("hTq%d" % i) for i in range(4)]
            QT = [sb(S, "a_QT%d" % h, [128, 2048], BF16) for h in range(4)]
            KT = [[sb(S, "a_KT%d_%d" % (i, h), [128, 2048], BF16) for h in range(4)] for i in range(2)]
            VA_ = [sb(S, "a_V%d" % i, [128, 16, 4, 130], BF16) for i in range(2)]
            stg = [sb(S, "a_stg%d" % i, [128, 512], F32) for i in range(2)]
            NBS = 2
            expS = [sb(S, "a_expS%d" % i, [128, 4, 256], F32) for i in range(NBS)]
            Pm = [sb(S, "a_P%d" % i, [128, 4, 256], BF16) for i in range(NBS)]
            Osb = [sb(S, "a_O%d" % i, [128, 4, 129], F32) for i in range(NBS)]
            for i in range(2):
                pool(lambda e, i=i: e.memset(VA_[i][:, :, :, 128:130], 1.0), w=[VA_[i]])
            ncls = dil
            nper = 2048 // dil
            nbc = nper // 128
            rows = (NPRE - 2048, NPRE, NPRE + 2048)

            def hquarters(sti):
                return (3,) if (sti == 0 and dil <= 4) else (0, 1, 2, 3)

            def hload(sti):
                row0 = rows[sti]
                for q4 in hquarters(sti):
                    dma(lambda e, q4=q4: e.dma_start(out=hT[:, :, q4 * 512:(q4 + 1) * 512],
                                                     in_=hTv[:, :, row0 + q4 * 512:row0 + (q4 + 1) * 512]), w=[hTb[q4]])

            hs = sb(S, "s_hT", [128, 8, NS], BF16)
            qTs = sb(S, "s_qT", [128, 4, NS], BF16)
            kTs = sb(S, "s_kT", [128, 4, NS], BF16)
            kc = [sb(S, "s_kc%d" % i, [128, 512], F32) for i in range(2)]
            vc = [sb(S, "s_vc%d" % i, [128, 512], F32) for i in range(2)]
            kcT = [sb(S, "s_kcT%d" % i, [128, 4, 128], BF16) for i in range(2)]
            vca = [sb(S, "s_vca%d" % i, [128, 4, 130], BF16) for i in range(2)]
            vna = [sb(S, "s_vna%d" % i, [8, 4, 130], BF16) for i in range(2)]
            kvst = sb(S, "s_kvst", [NS, 2, 512], F32)
            vall = sb(S, "s_vall", [NS, 4, 130], BF16)
            ex1 = [sb(S, "s_ex1%d" % i, [128, 4, 8], F32) for i in range(2)]
            ex2 = [sb(S, "s_ex2%d" % i, [8, 4, 8], F32) for i in range(2)]
            p1s = [sb(S, "s_p1%d" % i, [128, 4, 8], BF16) for i in range(2)]
            p2s = [sb(S, "s_p2%d" % i, [8, 4, 8], BF16) for i in range(2)]
            os_ = [sb(S, "s_o%d" % i, [8, 4, 129], F32) for i in range(2)]
            for i in range(2):
                pool(lambda e, i=i: e.memset(vca[i][:, :, 128:130], 1.0), w=[vca[i]])
            nq = 8 // dil if dil <= 8 else 1
            ncl = min(dil, 8)
            hload(0)
            def sample_setup():
                dma(lambda e: e.dma_start(out=hs[:], in_=hTv[:, :, SOFF:SOFF + NS]), w=[hs])
                for (dst, Wt) in ((qTs, Wq), (kTs, Wk)):
                    p = getF()
                    for h in range(4):
                        for k in range(8):
                            mm(p[:, h * NS:(h + 1) * NS], Wt[:, k, h * 128:(h + 1) * 128], hs[:, k, :], [Wt, hs], [p], start=(k == 0), stop=(k == 7))
                    act(lambda e, p=p, dst=dst: e.copy(out=dst[:], in_=p[:, 0:4 * NS].rearrange("p (h t) -> p h t", h=4)), r=[p], w=[dst])

                pool(lambda e: e.memset(vall[:, :, 128:130], 1.0), w=[vall])
                for kv, Wt in ((0, Wk), (1, Wv)):
                    p = getF()
                    for k in range(8):
                        mm(p[0:NS, :], hs[:, k, :], Wt[:, k, :], [hs, Wt], [p], start=(k == 0), stop=(k == 7))
                    act(lambda e, p=p, kv=kv: e.copy(out=kvst[:, kv, :], in_=p[0:NS, :]), r=[p], w=[kvst])
                pool(lambda e: e.tensor_copy(out=vall[:, :, 0:128], in_=kvst[:, 1, :].rearrange("p (h d) -> p h d", h=4)), r=[kvst], w=[vall])
                dma(lambda e: e.dma_start(out=ws_o[g], in_=kvst[:]), r=[kvst])


            def sample_unit(b, r, i2):
                tcol = b * 8 + r
                tsl = slice(tcol, tcol + dil * (nq - 1) + 1, dil)
                dma(lambda e: e.dma_start(out=kc[i2][:], in_=cw[g][b, r:r + dil * 127 + 1:dil, 0, :]), w=[kc[i2]])
                dma(lambda e: e.dma_start(out=vc[i2][:], in_=cw[g][b, r:r + dil * 127 + 1:dil, 1, :]), w=[vc[i2]])
                yield
                p = getF()
                for h in range(4):
                    pe(lambda e, p=p, h=h: e.transpose(out=p[:, h * 128:(h + 1) * 128], in_=kc[i2][:, h * 128:(h + 1) * 128],
                                                       identity=ident_f[:]), r=[kc[i2], ident_f], w=[p])
                act(lambda e, p=p: e.copy(out=kcT[i2][:], in_=p[:, :].rearrange("p (h n) -> p h n", h=4)), r=[p], w=[kcT[i2]])
                pool(lambda e: e.tensor_copy(out=vca[i2][:, :, 0:128], in_=vc[i2][:].rearrange("p (h d) -> p h d", h=4)),
                     r=[vc[i2]], w=[vca[i2]])
                dma(lambda e: e.dma_start(out=vna[i2][0:nq, :, :], in_=vall[tsl, :, :]), r=[vall], w=[vna[i2]])
                yield
                pa = getS()
                pb_ = getS()
                for h in range(4):
                    mm(pa[:, h * 8:h * 8 + nq], kcT[i2][:, h, :], qTs[:, h, tsl], [kcT[i2], qTs], [pa])
                    mm(pb_[0:nq, h * 8:h * 8 + nq], kTs[:, h, tsl], qTs[:, h, tsl], [kTs, qTs], [pb_])
                act(lambda e, pa=pa: e.activation(out=ex1[i2][:, :, 0:nq], in_=pa[:, 0:32].rearrange("p (h q) -> p h q", h=4)[:, :, 0:nq],
                                                  func=AF.Exp, scale=SC_A), r=[pa], w=[ex1[i2]])
                act(lambda e, pb_=pb_: e.activation(out=ex2[i2][0:nq, :, 0:nq],
                                                    in_=pb_[0:nq, 0:32].rearrange("p (h q) -> p h q", h=4)[:, :, 0:nq],
                                                    func=AF.Exp, scale=SC_A), r=[pb_], w=[ex2[i2]])
                yield
                pool(lambda e: e.tensor_tensor(out=p1s[i2][:, :, 0:nq], in0=ex1[i2][:, :, 0:nq], in1=Eg[g][:, :, 128:128 + nq], op=ALU.mult),
                     r=[ex1[i2], Eg[g]], w=[p1s[i2]])
                pool(lambda e: e.tensor_tensor(out=p2s[i2][0:nq, :, 0:nq], in0=ex2[i2][0:nq, :, 0:nq], in1=Eg[g][0:nq, :, 0:nq], op=ALU.mult),
                     r=[ex2[i2], Eg[g]], w=[p2s[i2]])
                yield
                for hp in range(2):
                    p = getF()
                    for hh in range(2):
                        h = hp * 2 + hh
                        mm(p[0:nq, hh * 256:hh * 256 + 129], p1s[i2][:, h, 0:nq], vca[i2][:, h, 0:129], [p1s[i2], vca[i2]], [p],
                           start=True, stop=False)
                        mm(p[0:nq, hh * 256:hh * 256 + 129], p2s[i2][0:nq, h, 0:nq], vna[i2][0:nq, h, 0:129], [p2s[i2], vna[i2]], [p],
                           start=False, stop=True)
                    act(lambda e, p=p, hp=hp: e.copy(out=os_[i2][0:nq, hp * 2:hp * 2 + 2, :],
                                                     in_=p[0:nq, :].rearrange("p (h c) -> p h c", h=2)[:, :, 0:129]), r=[p], w=[os_[i2]])
                t0 = NOWN + tcol
                dma(lambda e: e.dma_start(out=accd[g, t0:t0 + dil * (nq - 1) + 1:dil, :],
                                          in_=os_[i2][0:nq].rearrange("p h c -> p (h c)")), r=[os_[i2]])

            units = [(b, r) for b in range(4 if DBG_SAMPLE else 0) for r in range(ncl)]
            ucount = [0]

            def next_unit():
                if not units:
                    return None
                b, r = units.pop(0)
                i2 = ucount[0] % 2
                ucount[0] += 1
                return sample_unit(b, r, i2)

            def attn_block(sti, cur, prv, row0, r, nb, bi):
                blk = r * nbc + nb
                cs = r * nper + nb * 128
                if nb > 0:
                    pKT, pV, pblk, pcs = KT[cur], VA_[cur], blk - 1, cs - 128
                else:
                    pKT, pV, pblk, pcs = KT[prv], VA_[prv], r * nbc + nbc - 1, r * nper + (nbc - 1) * 128
                Et = Egf[g] if (sti == 1 and nb == 0) else Eg[g]
                ex = expS[bi % NBS]
                pm_ = Pm[bi % NBS]
                ot = Osb[bi % NBS]
                for hp in range(2):
                    p = getF()
                    for hh in range(2):
                        h = hp * 2 + hh
                        mm(p[:, hh * 256:hh * 256 + 128], KT[cur][h][:, cs:cs + 128], QT[h][:, cs:cs + 128], [KT[cur][h], QT[h]], [p])
                        mm(p[:, hh * 256 + 128:hh * 256 + 256], pKT[h][:, pcs:pcs + 128], QT[h][:, cs:cs + 128], [pKT[h], QT[h]], [p])
                    act(lambda e, p=p, hp=hp: e.activation(out=ex[:, hp * 2:hp * 2 + 2, :],
                                                           in_=p[:, :].rearrange("p (h c) -> p h c", h=2), func=AF.Exp, scale=SC_A),
                        r=[p], w=[ex])
                yield
                eng_ = dve if bi % 3 != 2 else pool
                eng_(lambda e: e.tensor_tensor(out=pm_[:], in0=ex[:], in1=Et[:], op=ALU.mult), r=[ex, Et], w=[pm_])
                yield
                for hp in range(2):
                    p = getF()
                    for hh in range(2):
                        h = hp * 2 + hh
                        mm(p[:, hh * 256:hh * 256 + 129], pm_[:, h, 0:128], VA_[cur][:, blk, h, 0:129], [pm_, VA_[cur]], [p],
                           start=True, stop=False)
                        mm(p[:, hh * 256:hh * 256 + 129], pm_[:, h, 128:256], pV[:, pblk, h, 0:129], [pm_, pV], [p],
                           start=False, stop=True)
                    act(lambda e, p=p, hp=hp: e.copy(out=ot[:, hp * 2:hp * 2 + 2, :],
                                                     in_=p[:, :].rearrange("p (h c) -> p h c", h=2)[:, :, 0:129]), r=[p], w=[ot])
                tok0 = (row0 - NPRE) + r + dil * 128 * nb
                dma(lambda e: e.dma_start(out=accd[g, tok0:tok0 + dil * 127 + 1:dil, :],
                                          in_=ot[:].rearrange("p h c -> p (h c)")), r=[ot])

            stgc = 0
            bcount = 0
            for sti, row0 in enumerate(rows if DBG_PROMPT else ()):
                own = sti > 0
                cur = sti % 2
                prv = 1 - cur
                hq = hquarters(sti)
                for (dstl, Wt, need) in ((KT[cur], Wk, True), (QT, Wq, own)):
                    if not need:
                        continue
                    for h in range(4):
                        dst = dstl[h]
                        for s_ in hq:
                            p = getF()
                            for k in range(8):
                                mm(p[:, :], Wt[:, k, h * 128:(h + 1) * 128], hT[:, k, s_ * 512:(s_ + 1) * 512], [Wt, hTb[s_]], [p],
                                   start=(k == 0), stop=(k == 7))
                            npc = 512 // dil
                            o_ap = dst[:, :].rearrange("p (r n) -> p r n", r=dil)[:, :, npc * s_:npc * (s_ + 1)]
                            i_ap = p[:, :].rearrange("p (m r) -> p r m", r=dil)
                            if h % 2 == 0:
                                act(lambda e, o_ap=o_ap, i_ap=i_ap: e.copy(out=o_ap, in_=i_ap), r=[p], w=[dst])
                            else:
                                dve(lambda e, o_ap=o_ap, i_ap=i_ap: e.tensor_copy(out=o_ap, in_=i_ap), r=[p], w=[dst])
                for r in range(ncls):
                    for nb in range(nbc):
                        if sti == 0 and nb != nbc - 1:
                            continue
                        blk = r * nbc + nb
                        u0 = r + dil * 128 * nb
                        qs = sorted(set([u0 // 512, (u0 + dil * 127) // 512])) if dil <= 4 else [0, 1, 2, 3]
                        hdeps = [hTb[q] for q in range(qs[0], qs[-1] + 1)]
                        p = getF()
                        for k in range(8):
                            mm(p[:, :], hT[:, k, u0:u0 + dil * 127 + 1:dil], Wv[:, k, :], hdeps + [Wv], [p], start=(k == 0), stop=(k == 7))
                        act(lambda e, p=p, blk=blk: e.copy(out=VA_[cur][:, blk, :, 0:128],
                                                           in_=p[:, :].rearrange("p (h d) -> p h d", h=4)), r=[p], w=[VA_[cur]])
                        tok0 = (row0 - NPRE) + u0
                        if sti == 2 and tok0 >= NOWN - W and DBG_W:
                            orow = tok0 - (NOWN - W)
                            s1 = stg[stgc % 2]
                            stgc += 1
                            act(lambda e, p=p, s1=s1: e.copy(out=s1[:], in_=p[:, :]), r=[p], w=[s1])
                            dma(lambda e, s1=s1, orow=orow: e.dma_start(out=wp_o[g][orow:orow + dil * 127 + 1:dil, 1, :], in_=s1[:]), r=[s1])
                            p2 = getF()
                            for k in range(8):
                                mm(p2[:, :], hT[:, k, u0:u0 + dil * 127 + 1:dil], Wk[:, k, :], hdeps + [Wk], [p2], start=(k == 0), stop=(k == 7))
                            s2 = stg[stgc % 2]
                            stgc += 1
                            act(lambda e, p2=p2, s2=s2: e.copy(out=s2[:], in_=p2[:, :]), r=[p2], w=[s2])
                            dma(lambda e, s2=s2, orow=orow: e.dma_start(out=wp_o[g][orow:orow + dil * 127 + 1:dil, 0, :], in_=s2[:]), r=[s2])
                if sti + 1 < len(rows):
                    hload(sti + 1)
                if not own:
                    continue
                if sti == 1:
                    sample_setup()
                gens = []
                for r in range(ncls):
                    for nb in range(nbc):
                        gens.append(attn_block(sti, cur, prv, row0, r, nb, bcount))
                        bcount += 1
                        u_ = next_unit()
                        if u_ is not None:
                            gens.append(u_)
                run_pipeline(gens, 3, 1)
            rest = []
            while True:
                u_ = next_unit()
                if u_ is None:
                    break
                rest.append(u_)
            run_pipeline(rest, 2, 2)
            P.barrier()

        for g, (W, dil) in enumerate(GROUPS):
            if g not in DBG_GROUPS:
                continue
            with contextlib.ExitStack() as S:
                attn_group(g, W, dil, S)

        if STOP == 3:
            GE.close()
            P.emit()
            return nc
        GE.close()
        with contextlib.ExitStack() as S:
            Wpa = sb(S, "c_Wpa", [128, 4, 1024], BF16)
            Wout = sb(S, "c_Wout", [128, 8, 1024], BF16)
            Wmq = sb(S, "c_Wmq", [128, 8, 1024], BF16)
            Wmo = sb(S, "c_Wmo", [128, 8, 1024], BF16)
            gpost = {n: sb(S, "c_g_" + n, [128, D], F32) for n in ("norm_mix_post", "norm_mem_post")}
            for n in gpost:
                dma(lambda e, n=n: e.dma_start(out=gpost[n][:], in_=nrm[n][0:1, :].broadcast_to([128, D])), w=[gpost[n]])
            KmT = sb(S, "c_KmT", [128, 8, 256], BF16)
            Vm = sb(S, "c_Vm", [128, 2, 1024], BF16)
            junk = sb(S, "c_junk", [128, D], BF16)
            class Set4:
                pass
            sets = []
            S3rd = contextlib.ExitStack()

            def mkset(S_):
                B_ = Set4()
                big = sb(S_, "c_big", [128, 1548], F32)
                B_.acc = View(big, lambda big=big: big[:, :].rearrange("p (g f) -> p g f", g=3))
                B_.msb = View(big, lambda big=big: big[:, 0:1024])
                B_.num = sb(S_, "c_num", [128, 4, 129], F32)
                B_.rden = sb(S_, "c_rden", [128, 4], F32)
                B_.comb = sb(S_, "c_comb", [128, 512], BF16)
                B_.combT = sb(S_, "c_combT", [128, 4, 128], BF16)
                B_.sga = sb(S_, "c_sga", [128, 8, 128], BF16)
                B_.x1 = sb(S_, "c_x1", [128, D], F32)
                B_.part = View(B_.x1, lambda x1=B_.x1: x1[:, :].rearrange("p (c t) -> p c t", c=8))
                B_.mixT = sb(S_, "c_mixT", [128, 8, 128], BF16)
                B_.h2T = B_.mixT
                B_.onT = B_.mixT
                B_.qmT = sb(S_, "c_qmT", [128, 8, 128], BF16)
                B_.Pmm = sb(S_, "c_Pmm", [128, 8, 128], BF16)
                B_.dens = sb(S_, "c_dens", [128, 4], F32)
                B_.xt = sb(S_, "c_xt", [128, D], F32)
                B_.xn = sb(S_, "c_xn2", [128, D], BF16)
                B_.onb = B_.xn
                B_.ss = sb(S_, "c_ss2", [128, 1], F32)
                B_.rstd = sb(S_, "c_rstd2", [128, 1], F32)
                sets.append(B_)
            mkset(S)
            mkset(S)

            with contextlib.ExitStack() as S2:
                Wmk = sb(S2, "c_Wmk", [128, 8, 1024], BF16)
                Wmv = sb(S2, "c_Wmv", [128, 8, 1024], BF16)
                dma(lambda e: e.dma_start(out=Wmk[:], in_=wview("w_mk")), w=[Wmk], q="pool")
                dma(lambda e: e.dma_start(out=Wmv[:], in_=wview("w_mv")), w=[Wmv], q="pool")
                dma(lambda e: e.dma_start(out=Wpa[:], in_=wview("w_proj_a")), w=[Wpa], q="pool")
                dma(lambda e: e.dma_start(out=Wout[:], in_=wview("w_out")), w=[Wout], q="pool")
                dma(lambda e: e.dma_start(out=Wmq[:], in_=wview("w_mq")), w=[Wmq], q="pool")
                dma(lambda e: e.dma_start(out=Wmo[:], in_=wview("w_mo")), w=[Wmo], q="pool")
                mT = sb(S2, "c_mT", [128, 8, 256], BF16)
                kvs = sb(S2, "c_kvs", [128, 2, 1024], F32)
                for mb in range(2):
                    Bm = sets[mb]
                    dma(lambda e, mb=mb, Bm=Bm: e.dma_start(out=Bm.xt[:], in_=mem[mb * 128:(mb + 1) * 128, :]), w=[Bm.xt])
                    norm_transpose(Bm.xt, 128, "norm_memtok", Bm.xn, Bm.ss, Bm.rstd, junk, mT, mb * 128)
                for mb in range(2):
                    for kv, Wt in ((0, Wmk), (1, Wmv)):
                        for half in range(2):
                            p = getF()
                            for k in range(8):
                                mm(p[:, :], mT[:, k, mb * 128:(mb + 1) * 128], Wt[:, k, half * 512:(half + 1) * 512], [mT, Wt], [p],
                                   start=(k == 0), stop=(k == 7))
                            act(lambda e, p=p, kv=kv, half=half: e.copy(out=kvs[:, kv, half * 512:(half + 1) * 512], in_=p[:, :]), r=[p], w=[kvs])
                            if kv == 1:
                                dve(lambda e, mb=mb, half=half: e.tensor_copy(out=Vm[:, mb, half * 512:(half + 1) * 512], in_=kvs[:, 1, half * 512:(half + 1) * 512]),
                                    r=[kvs], w=[Vm])
                    dma(lambda e, mb=mb: e.dma_start(out=memkv_o[mb * 128:(mb + 1) * 128, :, :], in_=kvs[:]), r=[kvs])
                for c in range(8):
                    p = getF()
                    for k in range(8):
                        mm(p[:, 0:256], Wmk[:, k, c * 128:(c + 1) * 128], mT[:, k, :], [Wmk, mT], [p], start=(k == 0), stop=(k == 7))
                    act(lambda e, p=p, c=c: e.copy(out=KmT[:, c, :], in_=p[:, 0:256]), r=[p], w=[KmT])
                P.barrier()
            mkset(S3rd)
            mkset(S3rd)
            def post_norm_res(B_, src_tiles, nt, gname, resid, dst):
                msb = B_.msb
                for half, p in enumerate(src_tiles):
                    act(lambda e, p=p, half=half: e.copy(out=msb[0:nt, half * 512:(half + 1) * 512], in_=p[0:nt, :]), r=[p], w=[msb])
                rms_stats("", msb[0:nt, :], nt, B_.ss, B_.rstd, junk, [msb], D)
                pool(lambda e: e.tensor_tensor(out=msb[0:nt, :], in0=msb[0:nt, :], in1=gpost[gname][0:nt, :], op=ALU.mult),
                     r=[msb, gpost[gname]], w=[msb])
                dve(lambda e: e.scalar_tensor_tensor(out=dst[0:nt, :], in0=msb[0:nt, :], scalar=B_.rstd[0:nt, 0:1], in1=resid[0:nt, :],
                                                     op0=ALU.mult, op1=ALU.add), r=[msb, B_.rstd, resid], w=[dst])

            def cross_attend(B_, KT_, V_, c0, nt):
                qmT, Pmm, omsb, dens, onb, onT = B_.qmT, B_.Pmm, B_.msb, B_.dens, B_.onb, B_.onT
                for hp in range(2):
                    p = getF()
                    for hh in range(2):
                        h = hp * 2 + hh
                        for mb in range(2):
                            col = (hh * 2 + mb) * nt
                            for dc in range(2):
                                mm(p[:, col:col + nt], KT_[:, h * 2 + dc, mb * 128:(mb + 1) * 128], qmT[:, h * 2 + dc, c0:c0 + nt], [KT_, qmT], [p],
                                   start=(dc == 0), stop=(dc == 1))
                    act(lambda e, p=p, hp=hp: e.activation(out=Pmm[:, hp * 4:hp * 4 + 4, 0:nt], in_=p[:, 0:4 * nt].rearrange("p (a t) -> p a t", a=4),
                                                           func=AF.Exp, scale=SC_M), r=[p], w=[Pmm])
                yield
                pd = getS()
                for hp in range(2):
                    p = getF()
                    for hh in range(2):
                        h = hp * 2 + hh
                        for mb in range(2):
                            mm(p[0:nt, hh * 256:(hh + 1) * 256], Pmm[:, h * 2 + mb, 0:nt], V_[:, mb, h * 256:(h + 1) * 256], [Pmm, V_], [p],
                               start=(mb == 0), stop=(mb == 1))
                    for hh in range(2):
                        h = hp * 2 + hh
                        for mb in range(2):
                            mm(pd[0:nt, h:h + 1], Pmm[:, h * 2 + mb, 0:nt], ones_b[:, 0:1], [Pmm, ones_b], [pd], start=(mb == 0), stop=(mb == 1))
                    act(lambda e, p=p, hp=hp: e.copy(out=omsb[0:nt, hp * 512:(hp + 1) * 512], in_=p[0:nt, :]), r=[p], w=[omsb])
                dve(lambda e, pd=pd: e.reciprocal(out=dens[0:nt, :], in_=pd[0:nt, 0:4]), r=[pd], w=[dens])
                yield
                dve(lambda e: e.tensor_tensor(out=onb[0:nt, :].rearrange("p (h e) -> p h e", h=4), in0=omsb[0:nt, :].rearrange("p (h e) -> p h e", h=4),
                                              in1=dens[0:nt, :].unsqueeze(2).broadcast_to([nt, 4, 256]), op=ALU.mult), r=[omsb, dens], w=[onb])
                yield
                pt = getB()
                for c in range(8):
                    pe(lambda e, c=c, pt=pt: e.transpose(out=pt[:, c * nt:(c + 1) * nt], in_=onb[0:nt, c * 128:(c + 1) * 128],
                                                          identity=ident_b[0:nt, 0:nt]), r=[onb, ident_b], w=[pt])
                act(lambda e, pt=pt: e.copy(out=onT[:, :, c0:c0 + nt], in_=pt[:, 0:8 * nt].rearrange("p (c t) -> p c t", c=8)), r=[pt], w=[onT])

            def tile4(ti, B_, smp_bufs=None):
                smp = ti == NOWN // 128
                nt = NS if smp else 128
                t0 = ti * 128
                hcol = SOFF if smp else NPRE + t0
                acc, num, rden, comb, combT, sga, part, mixT = B_.acc, B_.num, B_.rden, B_.comb, B_.combT, B_.sga, B_.part, B_.mixT
                xt, x1, h2T, qmT, onT = B_.xt, B_.x1, B_.h2T, B_.qmT, B_.onT
                dma(lambda e: e.dma_start(out=acc[0:nt], in_=accd.rearrange("g t f -> t g f")[t0:t0 + nt]), w=[acc])
                dma(lambda e: e.dma_start(out=sga[:, :, 0:nt], in_=sgav[:, :, t0:t0 + nt]), w=[sga])
                dma(lambda e: e.dma_start(out=part[:, :, 0:nt], in_=mixpv[:, :, t0:t0 + nt]), w=[part])
                if smp:
                    dma(lambda e: e.dma_start(out=xt[0:NS, :], in_=xs), w=[xt])
                else:
                    dma(lambda e: e.dma_start(out=xt[:], in_=xp[NPRE + t0:NPRE + t0 + 128, :]), w=[xt])
                yield
                dve(lambda e: e.tensor_tensor(out=num[0:nt].rearrange("p h c -> p (h c)"), in0=acc[0:nt, 0, :], in1=acc[0:nt, 1, :], op=ALU.add),
                    r=[acc], w=[num])
                dve(lambda e: e.tensor_tensor(out=num[0:nt].rearrange("p h c -> p (h c)"), in0=num[0:nt].rearrange("p h c -> p (h c)"),
                                              in1=acc[0:nt, 2, :], op=ALU.add), r=[acc, num], w=[num])
                dve(lambda e: e.reciprocal(out=rden[0:nt, :], in_=num[0:nt, :, 128]), r=[num], w=[rden])
                dve(lambda e: e.tensor_tensor(out=comb[0:nt, :].rearrange("p (h d) -> p h d", h=4), in0=num[0:nt, :, 0:128],
                                              in1=rden[0:nt, :].unsqueeze(2).broadcast_to([nt, 4, 128]), op=ALU.mult), r=[num, rden], w=[comb])
                yield
                pt = getB()
                for c in range(4):
                    pe(lambda e, c=c, pt=pt: e.transpose(out=pt[:, c * nt:(c + 1) * nt], in_=comb[0:nt, c * 128:(c + 1) * 128],
                                                          identity=ident_b[0:nt, 0:nt]), r=[comb, ident_b], w=[pt])
                act(lambda e, pt=pt: e.copy(out=combT[:, :, 0:nt], in_=pt[:, 0:4 * nt].rearrange("p (c t) -> p c t", c=4)), r=[pt], w=[combT])
                yield
                for q4 in range(2):
                    p = getF()
                    for cc in range(4):
                        c = q4 * 4 + cc
                        for k in range(4):
                            mm(p[:, cc * nt:(cc + 1) * nt], Wpa[:, k, c * 128:(c + 1) * 128], combT[:, k, 0:nt], [Wpa, combT], [p], start=(k == 0), stop=(k == 3))
                    dve(lambda e, p=p, q4=q4: e.tensor_tensor(out=mixT[:, q4 * 4:(q4 + 1) * 4, 0:nt], in0=p[:, 0:4 * nt].rearrange("p (c t) -> p c t", c=4),
                                                              in1=sga[:, q4 * 4:(q4 + 1) * 4, 0:nt], op=ALU.mult), r=[p, sga], w=[mixT])
                pool(lambda e: e.tensor_tensor(out=mixT[:, :, 0:nt], in0=mixT[:, :, 0:nt], in1=part[:, :, 0:nt], op=ALU.add), r=[mixT, part], w=[mixT])
                yield
                ph = []
                for half in range(2):
                    p = getF()
                    for k in range(8):
                        mm(p[0:nt, :], mixT[:, k, 0:nt], Wout[:, k, half * 512:(half + 1) * 512], [mixT, Wout], [p], start=(k == 0), stop=(k == 7))
                    ph.append(p)
                post_norm_res(B_, ph, nt, "norm_mix_post", xt, x1)
                yield
                norm_part1(x1, nt, B_.xn, B_.ss, B_.rstd, junk)
                yield
                yield
                norm_part2(nt, "norm_mem_pre", B_.xn, h2T, 0)
                yield
                for q4 in range(2):
                    p = getF()
                    for cc in range(4):
                        c = q4 * 4 + cc
                        for k in range(8):
                            mm(p[:, cc * nt:(cc + 1) * nt], Wmq[:, k, c * 128:(c + 1) * 128], h2T[:, k, 0:nt], [Wmq, h2T], [p], start=(k == 0), stop=(k == 7))
                    act(lambda e, p=p, q4=q4: e.copy(out=qmT[:, q4 * 4:(q4 + 1) * 4, 0:nt], in_=p[:, 0:4 * nt].rearrange("p (c t) -> p c t", c=4)),
                        r=[p], w=[qmT])
                yield
                if not smp:
                    for _ in cross_attend(B_, KmT, Vm, 0, 128):
                        yield
                else:
                    kms, vms, KmTs, Vms = smp_bufs
                    for b in range(4):
                        for mb in range(2):
                            dma(lambda e, b=b, mb=mb: e.dma_start(out=kms[:], in_=cmem[b, mb * 128:(mb + 1) * 128, 0, :]), w=[kms])
                            dma(lambda e, b=b, mb=mb: e.dma_start(out=vms[:], in_=cmem[b, mb * 128:(mb + 1) * 128, 1, :]), w=[vms])
                            for q4 in range(2):
                                p = getF()
                                for cc in range(4):
                                    c = q4 * 4 + cc
                                    pe(lambda e, p=p, c=c, cc=cc: e.transpose(out=p[:, cc * 128:(cc + 1) * 128], in_=kms[:, c * 128:(c + 1) * 128],
                                                                               identity=ident_f[:]), r=[kms, ident_f], w=[p])
                                act(lambda e, p=p, q4=q4, mb=mb: e.copy(out=KmTs[:, q4 * 4:(q4 + 1) * 4, mb * 128:(mb + 1) * 128],
                                                                        in_=p[:, :].rearrange("p (c m) -> p c m", c=4)), r=[p], w=[KmTs])
                            dve(lambda e, mb=mb: e.tensor_copy(out=Vms[:, mb, :], in_=vms[:]), r=[vms], w=[Vms])
                        for _ in cross_attend(B_, KmTs, Vms, b * 8, 8):
                            pass
                yield
                ph = []
                for half in range(2):
                    p = getF()
                    for k in range(8):
                        mm(p[0:nt, :], onT[:, k, 0:nt], Wmo[:, k, half * 512:(half + 1) * 512], [onT, Wmo], [p], start=(k == 0), stop=(k == 7))
                    ph.append(p)
                post_norm_res(B_, ph, nt, "norm_mem_post", x1, B_.msb)
                dma(lambda e: e.dma_start(out=x2d[t0:t0 + nt, :], in_=B_.msb[0:nt, :]), r=[B_.msb])

            run_pipeline([tile4(ti, sets[ti % 4]) for ti in range(NOWN // 128)], 4, 4)
            P.barrier()
            S3rd.close()
            with contextlib.ExitStack() as S3:
                kms = sets[1].msb
                vms = sets[1].x1
                KmTs = sb(S3, "c_KmTs", [128, 8, 256], BF16)
                Vms = sb(S3, "c_Vms", [128, 2, 1024], BF16)
                for _ in tile4(NOWN // 128, sets[0], (kms, vms, KmTs, Vms)):
                    pass
                P.barrier()
        if STOP == 4:
            GE.close()
            P.emit()
            return nc
        with contextlib.ExitStack() as S:
            Wg_ = sb(S, "f_Wg", [128, 8, DFF], BF16)
            Wu_ = sb(S, "f_Wu", [128, 8, DFF], BF16)
            Wd_ = sb(S, "f_Wd", [128, 22, D], BF16)
            WgB = [Buf("WgB%d" % i) for i in range(4)]
            WuB = [Buf("WuB%d" % i) for i in range(4)]
            WdB = [Buf("WdB%d" % i) for i in range(2)]
            for q in range(4):
                c0, c1 = q * 704, (q + 1) * 704
                dma(lambda e, c0=c0, c1=c1: e.dma_start(out=Wg_[:, :, c0:c1], in_=wview("w_ffn_gate")[:, :, c0:c1]), w=[WgB[q]], q="pool")
                dma(lambda e, c0=c0, c1=c1: e.dma_start(out=Wu_[:, :, c0:c1], in_=wview("w_ffn_up")[:, :, c0:c1]), w=[WuB[q]], q="pool")
            for q in range(2):
                dma(lambda e, q=q: e.dma_start(out=Wd_[:, q * 11:(q + 1) * 11, :], in_=wview("w_ffn_down")[:, q * 11:(q + 1) * 11, :]), w=[WdB[q]], q="pool")
            gpo = sb(S, "f_gpost", [128, D], F32)
            dma(lambda e: e.dma_start(out=gpo[:], in_=nrm["norm_ffn_post"][0:1, :].broadcast_to([128, D])), w=[gpo])
            xt = [sb(S, "f_x%d" % i, [128, D], F32) for i in range(2)]
            xns = [sb(S, "f_xn%d" % i, [128, D], BF16) for i in range(4)]
            junk = sb(S, "f_junk", [128, D], BF16)
            sss = [sb(S, "f_ss%d" % i, [128, 1], F32) for i in range(4)]
            rstds = [sb(S, "f_rstd%d" % i, [128, 1], F32) for i in range(4)]
            ss = sss[0]
            rstd = rstds[0]
            h3Ts = [sb(S, "f_h3T%d" % i, [128, 8, 512], BF16) for i in range(2)]
            actT = sb(S, "f_actT", [128, 22, 512], BF16)
            sgt = [sb(S, "f_sg%d" % i, [128, 512], F32) for i in range(2)]
            fsb = sb(S, "f_fsb", [128, D], F32)
            yt = sb(S, "f_y", [128, D], F32)
            xi = [0]

            def ffn_group(grp):
                smp = grp == NOWN // 512
                nsub = 1 if smp else 4
                ntok = NS if smp else 512
                h3T = h3Ts[grp % 2]
                for s_ in range(nsub):
                    nt = NS if smp else 128
                    t0 = grp * 512 + s_ * 128
                    x_t = xt[xi[0] % 2]
                    xi[0] += 1
                    dma(lambda e, x_t=x_t, t0=t0, nt=nt: e.dma_start(out=x_t[0:nt, :], in_=x2d[t0:t0 + nt, :]), w=[x_t])
                    norm_part1(x_t, nt, xns[s_], sss[s_], rstds[s_], junk)
                yield
                for s_ in range(nsub):
                    nt = NS if smp else 128
                    norm_part2(nt, "norm_ffn_pre", xns[s_], h3T, s_ * 128)
                yield
                for c in range(22):
                    if c == 11:
                        yield
                    pg = getF()
                    pu = getF()
                    qb = (c * 128) // 704
                    qe = (c * 128 + 127) // 704
                    for k in range(8):
                        mm(pg[:, 0:ntok], Wg_[:, k, c * 128:(c + 1) * 128], h3T[:, k, 0:ntok], [WgB[qb], WgB[qe], h3T], [pg], start=(k == 0), stop=(k == 7))
                    for k in range(8):
                        mm(pu[:, 0:ntok], Wu_[:, k, c * 128:(c + 1) * 128], h3T[:, k, 0:ntok], [WuB[qb], WuB[qe], h3T], [pu], start=(k == 0), stop=(k == 7))
                    sg_ = sgt[c % 2]
                    act(lambda e, pg=pg, sg_=sg_: e.activation(out=sg_[:, 0:ntok], in_=pg[:, 0:ntok], func=AF.Silu), r=[pg], w=[sg_])
                    dve(lambda e, pu=pu, sg_=sg_, c=c: e.tensor_tensor(out=actT[:, c, 0:ntok], in0=pu[:, 0:ntok], in1=sg_[:, 0:ntok], op=ALU.mult),
                        r=[pu, sg_], w=[actT])
                yield
                for s_ in range(nsub):
                    nt = NS if smp else 128
                    t0 = grp * 512 + s_ * 128
                    x_t = xt[xi[0] % 2]
                    xi[0] += 1
                    dma(lambda e, x_t=x_t, t0=t0, nt=nt: e.dma_start(out=x_t[0:nt, :], in_=x2d[t0:t0 + nt, :]), w=[x_t])
                    for half in range(2):
                        p = getF()
                        for c in range(22):
                            mm(p[0:nt, :], actT[:, c, s_ * 128:s_ * 128 + nt], Wd_[:, c, half * 512:(half + 1) * 512], [actT, WdB[c // 11]], [p],
                               start=(c == 0), stop=(c == 21))
                        act(lambda e, p=p, half=half, nt=nt: e.copy(out=fsb[0:nt, half * 512:(half + 1) * 512], in_=p[0:nt, :]), r=[p], w=[fsb])
                    rms_stats("", fsb[0:nt, :], nt, ss, rstd, junk, [fsb], D)
                    dve(lambda e, nt=nt: e.scalar_tensor_tensor(out=fsb[0:nt, :], in0=fsb[0:nt, :], scalar=rstd[0:nt, 0:1], in1=gpo[0:nt, :],
                                                                op0=ALU.mult, op1=ALU.mult), r=[fsb, rstd, gpo], w=[fsb])
                    pool(lambda e, nt=nt, x_t=x_t: e.tensor_tensor(out=yt[0:nt, :], in0=fsb[0:nt, :], in1=x_t[0:nt, :], op=ALU.add),
                         r=[fsb, x_t], w=[yt])
                    if smp:
                        dma(lambda e: e.dma_start(out=ys_o, in_=yt[0:NS, :]), r=[yt])
                    else:
                        dma(lambda e, t0=t0: e.dma_start(out=y_o[t0:t0 + 128, :], in_=yt[:]), r=[yt])

            run_pipeline([ffn_group(grp) for grp in range(NOWN // 512 + 1)], 2, 2)
        P.emit()
    return nc


_NC = None


def kernel(**inp):
    global _NC
    f32 = np.float32
    x_prompt = np.asarray(inp["x_prompt"], f32)
    x_sample = np.asarray(inp["x_sample"], f32)
    oh = onehot_tables()
    shared = {"oh": oh, "rel_bias": np.asarray(inp["rel_bias"], f32), "w_f2": np.asarray(inp["w_f2"], f32)[0],
              "b_f2": np.asarray(inp["b_f2"], f32)[0][None, :], "gla_norm": np.asarray(inp["gla_norm"], f32)[0][None, :]}
    for n in NORMS:
        shared[n] = np.asarray(inp[n], f32)[0][None, :]
    for (n, k, m) in WEIGHTS:
        shared[n] = np.ascontiguousarray(np.asarray(inp[n], f32)[0])
    in_maps = []
    for c in range(8):
        b, half = c // 2, c % 2
        if half == 1:
            xp = x_prompt[b]
        else:
            xp = np.concatenate([np.zeros((NPRE, D), f32), x_prompt[b, :NOWN]], axis=0)
        m = dict(shared)
        m["xp"] = np.ascontiguousarray(xp)
        m["xs"] = np.ascontiguousarray(x_sample[4 * c:4 * c + 4].reshape(NS, D))
        m["flag"] = np.full((128, 1), float(half), f32)
        m["cw1"] = np.ascontiguousarray(np.asarray(inp["cache_win1_kv"], f32)[0, 4 * c:4 * c + 4].reshape(4, 128, 2, 512))
        m["cw2"] = np.ascontiguousarray(np.asarray(inp["cache_win2_kv"], f32)[0, 4 * c:4 * c + 4].reshape(4, 512, 2, 512))
        m["cw3"] = np.ascontiguousarray(np.asarray(inp["cache_win3_kv"], f32)[0, 4 * c:4 * c + 4].reshape(4, 2048, 2, 512))
        m["sgla"] = np.ascontiguousarray(np.asarray(inp["state_gla"], f32)[0, 4 * c:4 * c + 4])
        m["cmem"] = np.ascontiguousarray(np.asarray(inp["cache_mem_kv"], f32)[0, 4 * c:4 * c + 4].reshape(4, 256, 2, 1024))
        m["mem"] = np.ascontiguousarray(np.asarray(inp["mem_prompt"], f32)[b])
        in_maps.append(m)
    if _NC is None:
        _NC = build()
    res = run_bass_kernel_spmd(_NC, in_maps, core_ids=list(range(8)))
    R = res.results
    y_prompt = np.zeros((4, 8192, D), f32)
    y_sample = np.zeros((32, 8, D), f32)
    w1p = np.zeros((1, 4, 128, 2, 4, 128), f32)
    w2p = np.zeros((1, 4, 512, 2, 4, 128), f32)
    w3p = np.zeros((1, 4, 2048, 2, 4, 128), f32)
    glap = np.zeros((1, 4, 4, 128, 256), f32)
    memkv = np.zeros((1, 4, 256, 2, 4, 256), f32)
    w1s = np.zeros((1, 32, 8, 2, 4, 128), f32)
    w2s = np.zeros((1, 32, 8, 2, 4, 128), f32)
    w3s = np.zeros((1, 32, 8, 2, 4, 128), f32)
    glas = np.zeros((1, 32, 4, 128, 256), f32)
    for c in range(8):
        b, half = c // 2, c % 2
        r = R[c]
        y_prompt[b, half * NOWN:(half + 1) * NOWN] = r["y"]
        y_sample[4 * c:4 * c + 4] = r["ys"].reshape(4, 8, D)
        if half == 1:
            w1p[0, b] = r["w1p"].reshape(128, 2, 4, 128)
            w2p[0, b] = r["w2p"].reshape(512, 2, 4, 128)
            w3p[0, b] = r["w3p"].reshape(2048, 2, 4, 128)
            glap[0, b] = r["glap"]
        else:
            memkv[0, b] = r["memkv"].reshape(256, 2, 4, 256)
        w1s[0, 4 * c:4 * c + 4] = r["w1s"].reshape(4, 8, 2, 4, 128)
        w2s[0, 4 * c:4 * c + 4] = r["w2s"].reshape(4, 8, 2, 4, 128)
        w3s[0, 4 * c:4 * c + 4] = r["w3s"].reshape(4, 8, 2, 4, 128)
        glas[0, 4 * c:4 * c + 4] = r["glas"]
    return (y_prompt, y_sample, w1p, w2p, w3p, glap, memkv, w1s, w2s, w3s, glas)
```

```python
import math
import types
import contextlib
import numpy as np
import concourse.bass as bass
import concourse.mybir as mybir
from concourse.bass_utils import run_bass_kernel_spmd

F32 = mybir.dt.float32
BF16 = mybir.dt.bfloat16
AF = mybir.ActivationFunctionType
ALU = mybir.AluOpType


class Buf:
    __slots__ = ("name", "last_w", "readers")

    def __init__(self, name=""):
        self.name = name
        self.last_w = None
        self.readers = []


class T:
    def __init__(self, t, name=""):
        self.t = t
        self.b = Buf(name)

    def __getitem__(self, k):
        return self.t[k]


class View:
    def __init__(self, parent, fn):
        self.b = parent.b
        self.fn = fn

    def __getitem__(self, k):
        return self.fn()[k]


class Op:
    __slots__ = ("eng", "fn", "is_dma", "deps", "marked", "ordinal", "dsem", "dval")

    def __init__(self, eng, fn, is_dma):
        self.eng = eng
        self.fn = fn
        self.is_dma = is_dma
        self.deps = []
        self.marked = False
        self.ordinal = 0
        self.dsem = None
        self.dval = 0


ENGS = ("pe", "act", "dve", "pool", "sp")
NDSEM = 8


def _b(x):
    return x.b if hasattr(x, "b") else x


def _freeze(fn):
    if fn.__closure__ is None:
        return fn
    cells = []
    for c in fn.__closure__:
        try:
            cells.append(types.CellType(c.cell_contents))
        except ValueError:
            cells.append(c)
    return types.FunctionType(fn.__code__, fn.__globals__, fn.__name__, fn.__defaults__, tuple(cells))


class Prog:
    def __init__(self, nc):
        self.nc = nc
        self.ops = {e: [] for e in ENGS}
        self.pending = {e: None for e in ENGS}

    def op(self, eng, fn, reads=(), writes=(), dma=False):
        o = Op(eng, _freeze(fn), dma)
        reads = [_b(x) for x in reads]
        writes = [_b(x) for x in writes]
        deps = {}
        for b in reads:
            w = b.last_w
            if w is not None:
                deps[id(w)] = (w, True)
        for b in writes:
            w = b.last_w
            if w is not None and id(w) not in deps:
                deps[id(w)] = (w, False)
            for r in b.readers:
                if id(r) not in deps:
                    deps[id(r)] = (r, False)
        for (d, raw) in deps.values():
            if d is o:
                continue
            if d.is_dma:
                o.deps.append(d)
                d.marked = True
                continue
            if d.eng == eng and not dma:
                if eng == "pe" or not raw:
                    continue
            o.deps.append(d)
            d.marked = True
        pb = self.pending[eng]
        if pb is not None:
            if eng != "sp":
                o.deps.append(pb)
            self.pending[eng] = None
        for b in writes:
            b.last_w = o
            b.readers = []
        for b in reads:
            if b.last_w is not o:
                if not dma:
                    b.readers = [r for r in b.readers if r.is_dma or r.eng != eng]
                b.readers.append(o)
        self.ops[eng].append(o)
        return o

    def barrier(self):
        deps = []
        for e in ENGS:
            lst = self.ops[e]
            for o in reversed(lst):
                if not o.is_dma:
                    deps.append(o)
                    break
            cnt = 0
            for o in reversed(lst):
                if o.is_dma:
                    deps.append(o)
                    cnt += 1
                    if cnt >= NDSEM:
                        break
        bo = Op("sp", lambda e: e.nop(), False)
        for d in deps:
            if d.eng == "sp" and not d.is_dma:
                continue
            bo.deps.append(d)
            d.marked = True
        bo.marked = True
        self.ops["sp"].append(bo)
        self.pending = {e: bo for e in ENGS}

    def emit(self):
        nc = self.nc
        with contextlib.ExitStack() as st:
            csem = {e: st.enter_context(nc.semaphore("c_" + e)) for e in ENGS}
            dsems = {e: [st.enter_context(nc.semaphore("d_%s%d" % (e, i))) for i in range(NDSEM)]
                     for e in ENGS if e != "pe"}
            for e in ENGS:
                k = 0
                nd = 0
                for o in self.ops[e]:
                    if o.is_dma:
                        o.dsem = (e, nd % NDSEM)
                        o.dval = 16 * (nd // NDSEM + 1)
                        nd += 1
                    elif o.marked:
                        k += 1
                        o.ordinal = k
            block = st.enter_context(nc.Block())
            engobj = {"pe": "tensor", "act": "scalar", "dve": "vector", "pool": "gpsimd", "sp": "sync"}

            def run(e, eng):
                seen_c = {x: 0 for x in ENGS}
                seen_d = {}
                for o in self.ops[e]:
                    for d in o.deps:
                        if d.is_dma:
                            key = d.dsem
                            if seen_d.get(key, 0) < d.dval:
                                eng.wait_ge(dsems[key[0]][key[1]], d.dval)
                                seen_d[key] = d.dval
                        else:
                            if seen_c[d.eng] < d.ordinal:
                                eng.wait_ge(csem[d.eng], d.ordinal)
                                seen_c[d.eng] = d.ordinal
                    if o.is_dma:
                        key = o.dsem
                        prev = o.dval - 16
                        if prev > 0 and seen_d.get(key, 0) < prev:
                            eng.wait_ge(dsems[key[0]][key[1]], prev)
                            seen_d[key] = prev
                        ins = o.fn(eng)
                        ins.then_inc(dsems[key[0]][key[1]], 16)
                    else:
                        ins = o.fn(eng)
                        if o.marked:
                            ins.then_inc(csem[e], 1)
                last = {}
                for o in self.ops[e]:
                    if o.is_dma:
                        last[o.dsem] = o.dval
                for key, v in last.items():
                    if seen_d.get(key, 0) < v:
                        eng.wait_ge(dsems[key[0]][key[1]], v)

            for e in ENGS:
                if not self.ops[e]:
                    continue

                def mk(e):
                    def f(eng):
                        run(e, eng)
                    return f
                getattr(block, engobj[e])(mk(e))


D = 1024
NPRE = 4096
NOWN = 4096
NS = 32
NTOK = NPRE + NOWN + NS
SOFF = NPRE + NOWN
INW = 9744
QA, KA, VA, QB, KB, VB, GB, FB, GA, GBT = 0, 1536, 3072, 4608, 5120, 5632, 6656, 7680, 7696, 8720
DFF = 2816
GROUPS = ((128, 1), (512, 4), (2048, 16))
EPS = 1e-6
SC_A = 128 ** -0.5
SC_B = 128 ** -0.5
SC_M = 256 ** -0.5


def t5_bucket_np(n):
    n = np.asarray(n, dtype=np.int64)
    nf = np.maximum(n, 1).astype(np.float32)
    large = 16 + (np.log(nf / np.float32(16)) / np.float32(math.log(2048 / 16)) * np.float32(16)).astype(np.int32)
    return np.where(n < 16, n, np.minimum(large, 31))


def onehot_tables():
    oh = np.zeros((3, 32, 129), np.float32)
    for g, (w, dil) in enumerate(GROUPS):
        bk = t5_bucket_np(np.arange(129) * dil)
        for d in range(129):
            oh[g, bk[d], d] = 1.0
    return oh


WEIGHTS = [("w_in", D, INW), ("w_proj_a", 512, D), ("w_proj_b", D, D), ("w_out", D, D), ("w_mk", D, D),
           ("w_mv", D, D), ("w_mq", D, D), ("w_mo", D, D), ("w_ffn_gate", D, DFF), ("w_ffn_up", D, DFF),
           ("w_ffn_down", DFF, D)]
NORMS = ["norm_mix_pre", "norm_mix_post", "norm_memtok", "norm_mem_pre", "norm_mem_post", "norm_ffn_pre",
         "norm_ffn_post"]


STOP = None
DBG_GROUPS = (0, 1, 2)
DBG_SAMPLE = True
DBG_PROMPT = True
DBG_SKIP12 = False
DBG_A = 9
PIPE_DEPTH = 2
PIPE_STAGGER = 6


def run_pipeline(gens, depth, stagger):
    it = iter(gens)
    active = []
    done = False
    while True:
        if not done and len(active) < depth and (not active or active[-1][1] >= stagger):
            try:
                active.append([next(it), 0])
            except StopIteration:
                done = True
        if not active:
            if done:
                break
            continue
        for a in list(active):
            try:
                next(a[0])
                a[1] += 1
            except StopIteration:
                active.remove(a)
DBG_W = True


def build():
    nc = bass.Bass("TRN2", target_bir_lowering=False)
    P = Prog(nc)

    def din(name, shape, dt=F32):
        return nc.dram_tensor(name, shape, dt, kind="ExternalInput").ap()

    def dout(name, shape):
        return nc.dram_tensor(name, shape, F32, kind="ExternalOutput").ap()

    def dscr(name, shape, dt):
        return nc.dram_tensor(name, shape, dt, kind="Internal").ap()

    xp = din("xp", [NPRE + NOWN, D])
    xs = din("xs", [NS, D])
    flag = din("flag", [128, 1])
    oh_d = din("oh", [3, 32, 129])
    cw = [din("cw1", [4, 128, 2, 512]), din("cw2", [4, 512, 2, 512]), din("cw3", [4, 2048, 2, 512])]
    sgla = din("sgla", [4, 4, 128, 256])
    cmem = din("cmem", [4, 256, 2, 1024])
    mem = din("mem", [256, D])
    relb = din("rel_bias", [32, 12])
    wf2 = din("w_f2", [16, 512])
    bf2 = din("b_f2", [1, 512])
    glan = din("gla_norm", [1, 256])
    nrm = {n: din(n, [1, D]) for n in NORMS}
    wsrc = {n: din(n, [k, m]) for (n, k, m) in WEIGHTS}

    y_o = dout("y", [NOWN, D])
    ys_o = dout("ys", [NS, D])
    wp_o = [dout("w1p", [128, 2, 512]), dout("w2p", [512, 2, 512]), dout("w3p", [2048, 2, 512])]
    glap_o = dout("glap", [4, 128, 256])
    memkv_o = dout("memkv", [256, 2, 1024])
    ws_o = [dout("w1s", [NS, 2, 512]), dout("w2s", [NS, 2, 512]), dout("w3s", [NS, 2, 512])]
    glas_o = dout("glas", [4, 4, 128, 256])

    hTd = dscr("hTd", [D, NTOK], BF16)
    accd = dscr("accd", [3, NOWN + NS, 516], F32)
    mixpd = dscr("mixpd", [D, NOWN + NS], F32)
    x2d = dscr("x2d", [NOWN + NS, D], F32)
    vvd = dscr("vvd", [3, 4, 384], F32)
    obTd = dscr("obTd", [D, NOWN + NS], BF16)
    sgad = dscr("sgad", [D, NOWN + NS], BF16)

    hTv = hTd.rearrange("(c p) t -> p c t", p=128)
    mixpv = mixpd.rearrange("(c p) t -> p c t", p=128)
    obTv = obTd.rearrange("(c p) t -> p c t", p=128)
    sgav = sgad.rearrange("(c p) t -> p c t", p=128)

    def wview(name):
        return wsrc[name].rearrange("(c p) n -> p c n", p=128)

    with contextlib.ExitStack() as G:
        uid = [0]

        def sb(st, name, shape, dt):
            uid[0] += 1
            name = "%s_%d" % (name, uid[0])
            return T(st.enter_context(nc.sbuf_tensor(name, shape, dt)), name)

        psF = [T(G.enter_context(nc.psum_tensor("psF%d" % i, [128, 512], F32)), "psF%d" % i) for i in range(6)]
        psB = [T(G.enter_context(nc.psum_tensor("psB%d" % i, [128, 1024], BF16)), "psB%d" % i) for i in range(2)]
        cnt = {"f": 0, "b": 0, "s": 0}

        def getS():
            return getF()

        def getF():
            cnt["f"] += 1
            return psF[cnt["f"] % 6]

        def getB():
            cnt["b"] += 1
            return psB[cnt["b"] % 2]

        ident_f = sb(G, "ident_f", [128, 128], F32)
        ident_b = sb(G, "ident_b", [128, 128], BF16)
        Jf = sb(G, "Jf", [128, 128], F32)
        Uf = sb(G, "Uf", [128, 128], F32)
        triU = sb(G, "triU", [128, 128], F32)
        triL = sb(G, "triL", [128, 128], F32)
        ones_f = sb(G, "ones_f", [128, 128], F32)
        ones_b = sb(G, "ones_b", [128, 2], BF16)
        m16 = sb(G, "m16", [128, 2], F32)
        flag_sb = sb(G, "flag_sb", [128, 1], F32)
        GE = contextlib.ExitStack()
        gT = {n: sb(G, "gT_" + n, [128, 8], F32) for n in ("norm_mix_pre", "norm_memtok", "norm_mem_pre", "norm_ffn_pre")}

        Eg = [sb(GE, "E%d" % g, [128, 4, 256], BF16) for g in range(3)]
        Egf = [sb(GE, "Ef%d" % g, [128, 4, 256], BF16) for g in range(3)]

        def pool(fn, r=(), w=()):
            return P.op("pool", fn, r, w)

        def dve(fn, r=(), w=()):
            return P.op("dve", fn, r, w)

        def act(fn, r=(), w=()):
            return P.op("act", fn, r, w)

        def pe(fn, r=(), w=()):
            return P.op("pe", fn, r, w)

        def dma(fn, r=(), w=(), q="sp"):
            return P.op(q, fn, r, w, dma=True)

        def mm(out, lhsT, rhs, r, w, start=True, stop=True):
            return pe(lambda e: e.matmul(out, lhsT=lhsT, rhs=rhs, start=start, stop=stop), r, w)

        with contextlib.ExitStack() as S:
            pool(lambda e: e.memset(ones_f[:], 1.0), w=[ones_f])
            pool(lambda e: e.memset(ones_b[:], 1.0), w=[ones_b])
            pool(lambda e: e.memset(m16[:], -1.0 / 16), w=[m16])
            pool(lambda e: e.memset(ident_f[:], 1.0), w=[ident_f])
            pool(lambda e: e.affine_select(out=ident_f[:], in_=ident_f[:], pattern=[[-1, 128]], compare_op=ALU.is_equal,
                                           fill=0.0, base=0, channel_multiplier=1), r=[ident_f], w=[ident_f])
            pool(lambda e: e.memset(Jf[:], 1.0), w=[Jf])
            pool(lambda e: e.affine_select(out=Jf[:], in_=Jf[:], pattern=[[1, 128]], compare_op=ALU.is_equal,
                                           fill=0.0, base=-127, channel_multiplier=1), r=[Jf], w=[Jf])
            pool(lambda e: e.memset(Uf[:], 1.0), w=[Uf])
            pool(lambda e: e.affine_select(out=Uf[:], in_=Uf[:], pattern=[[1, 128]], compare_op=ALU.is_ge,
                                           fill=0.0, base=0, channel_multiplier=-1), r=[Uf], w=[Uf])
            dve(lambda e: e.tensor_copy(out=ident_b[:], in_=ident_f[:]), r=[ident_f], w=[ident_b])
            dve(lambda e: e.tensor_scalar(out=triU[:], in0=Uf[:], scalar1=-1.0 / 16, scalar2=None, op0=ALU.mult), r=[Uf], w=[triU])
            dve(lambda e: e.tensor_scalar(out=triL[:], in0=Uf[:], scalar1=1.0 / 16, scalar2=-1.0 / 16, op0=ALU.mult, op1=ALU.add),
                r=[Uf], w=[triL])
            dma(lambda e: e.dma_start(out=flag_sb[:], in_=flag), w=[flag_sb])
            for n in gT:
                dma(lambda e, n=n: e.dma_start(out=gT[n][:], in_=nrm[n].rearrange("o (c p) -> p (o c)", p=128),
                                               allow_slow_non_contiguous=True), w=[gT[n]])
            tab = sb(S, "tab", [32, 12], F32)
            ohs = sb(S, "ohs", [32, 3, 129], F32)
            vv = sb(S, "vv", [4, 3, 384], F32)
            hk = sb(S, "hk", [128, 4, 256], F32)
            dma(lambda e: e.dma_start(out=tab[:], in_=relb), w=[tab])
            dma(lambda e: e.dma_start(out=ohs[:], in_=oh_d.rearrange("g b d -> b g d")), w=[ohs])
            pool(lambda e: e.memset(vv[:], 0.0), w=[vv])
            for g in range(3):
                p = getF()
                mm(p[0:4, 0:129], tab[:, 4 * g:4 * g + 4], ohs[:, g, :], [tab, ohs], [p])
                act(lambda e, g=g, p=p: e.activation(out=vv[:, g, 127:256], in_=p[0:4, 0:129], func=AF.Exp), r=[p, vv], w=[vv])
            dma(lambda e: e.dma_start(out=vvd.rearrange("g h m -> h g m"), in_=vv[:]), r=[vv], w=[])
            P.barrier()
            for g in range(3):
                src = bass.AP(vvd.tensor, g * 4 * 384, [[1, 128], [384, 4], [1, 256]])
                dma(lambda e, src=src: e.dma_start(out=hk[:], in_=src), w=[hk])
                for h in range(4):
                    p = getF()
                    mm(p[:, 0:256], Jf[:], hk[:, h, :], [Jf, hk], [p])
                    act(lambda e, g=g, h=h, p=p: e.copy(out=Eg[g][:, h, :], in_=p[:, 0:256]), r=[p], w=[Eg[g]])
                dve(lambda e, g=g: e.tensor_copy(out=Egf[g][:, :, 0:128], in_=Eg[g][:, :, 0:128]), r=[Eg[g]], w=[Egf[g]])
                dve(lambda e, g=g: e.tensor_scalar(out=Egf[g][:, :, 128:256], in0=Eg[g][:, :, 128:256], scalar1=flag_sb[:, 0:1],
                                                   scalar2=None, op0=ALU.mult), r=[Eg[g], flag_sb], w=[Egf[g]])
            P.barrier()

        if STOP == 0:
            GE.close()
            P.emit()
            return nc
        def rms_stats(st_tag, src_ap, nt, ss, rstd, junk, src_bufs, width):
            act(lambda e: e.activation(out=junk[0:nt, 0:width], in_=src_ap, func=AF.Square, accum_out=ss[0:nt, 0:1]),
                r=list(src_bufs) + [junk], w=[junk, ss])
            act(lambda e: e.activation(out=rstd[0:nt, :], in_=ss[0:nt, :], func=AF.Ln, scale=1.0 / width, bias=EPS), r=[ss], w=[rstd])
            act(lambda e: e.activation(out=rstd[0:nt, :], in_=rstd[0:nt, :], func=AF.Exp, scale=-0.5), r=[rstd], w=[rstd])

        def norm_part1(x_t, nt, xn, ss, rstd, junk):
            rms_stats("", x_t[0:nt, :], nt, ss, rstd, junk, [x_t], D)
            dve(lambda e: e.tensor_scalar(out=xn[0:nt, :], in0=x_t[0:nt, :], scalar1=rstd[0:nt, 0:1], scalar2=None, op0=ALU.mult),
                r=[x_t, rstd], w=[xn])

        def norm_part2(nt, gname, xn, dstT, c0):
            p = getB()
            for c in range(8):
                pe(lambda e, c=c, p=p: e.transpose(out=p[:, c * nt:(c + 1) * nt], in_=xn[0:nt, c * 128:(c + 1) * 128],
                                                    identity=ident_b[0:nt, 0:nt]), r=[xn, ident_b], w=[p])
            dve(lambda e, p=p: e.tensor_tensor(out=dstT[:, :, c0:c0 + nt], in0=p[:, 0:8 * nt].rearrange("p (c t) -> p c t", c=8),
                                               in1=gT[gname][:, :].unsqueeze(2).broadcast_to([128, 8, nt]), op=ALU.mult),
                r=[p, gT[gname]], w=[dstT])

        def norm_transpose(x_t, nt, gname, xn, ss, rstd, junk, dstT, c0):
            norm_part1(x_t, nt, xn, ss, rstd, junk)
            norm_part2(nt, gname, xn, dstT, c0)

        if STOP == 1:
            GE.close()
            P.emit()
            return nc
        with contextlib.ExitStack() as S:
            Wq = sb(S, "g_Wq", [128, 8, 512], BF16)
            Wk = sb(S, "g_Wk", [128, 8, 512], BF16)
            Wv = sb(S, "g_Wv", [128, 8, 1024], BF16)
            Wg = sb(S, "g_Wg", [128, 8, 1024], BF16)
            Wfb = sb(S, "g_Wfb", [128, 8, 16], BF16)
            wv_in = wview("w_in")
            for (wt, c0, n) in ((Wfb, FB, 16), (Wv, VB, 1024), (Wk, KB, 512), (Wq, QB, 512), (Wg, GB, 1024)):
                dma(lambda e, wt=wt, c0=c0, n=n: e.dma_start(out=wt[:], in_=wv_in[:, :, c0:c0 + n]), w=[wt], q="pool")
            wf2_sb = sb(S, "g_wf2", [17, 512], F32)
            gn_sb = sb(S, "g_gn", [128, 256], F32)
            dma(lambda e: e.dma_start(out=wf2_sb[0:16, :], in_=wf2), w=[wf2_sb])
            dma(lambda e: e.dma_start(out=wf2_sb[16:17, :], in_=bf2), w=[wf2_sb])
            dma(lambda e: e.dma_start(out=gn_sb[:], in_=glan[0:1, :].broadcast_to([128, 256])), w=[gn_sb])
            Sst = sb(S, "g_S", [128, 4, 256], F32)
            NSB = 5
            Sbfs = [sb(S, "g_Sbf%d" % i, [128, 4, 256], BF16) for i in range(NSB)]
            hsm = sb(S, "g_hsm", [128, 8, NS], BF16)
            junk = sb(S, "g_junk", [128, 256], BF16)
            junkD = sb(S, "g_junkD", [128, D], BF16)

            class Set2:
                pass
            NGS = 4
            gsets = []
            for si in range(NGS):
                B_ = Set2()
                B_.xt = sb(S, "g_xt", [128, D], F32)
                B_.xn = sb(S, "g_xn", [128, D], BF16)
                B_.ss = sb(S, "g_ss", [128, 1], F32)
                B_.rstd = sb(S, "g_rstd", [128, 1], F32)
                B_.hT = sb(S, "g_hTs", [128, 8, 128], BF16)
                B_.fbT = sb(S, "g_fbT", [32, 128], F32)
                pool(lambda e, t_=B_.fbT: e.memset(t_[:], 1.0), w=[B_.fbT])
                B_.spl = sb(S, "g_sp", [128, 512], F32)
                B_.Ed = sb(S, "g_Ed", [128, 512], F32)
                B_.t1 = B_.Ed
                B_.kd = sb(S, "g_kd", [128, 512], BF16)
                B_.vsb = sb(S, "g_v", [128, 1024], BF16)
                B_.eb = sb(S, "g_eb", [128, 4], F32)
                B_.Epos = sb(S, "g_Ep", [128, 512], F32)
                B_.Eneg = sb(S, "g_En", [128, 512], F32)
                B_.qeT = sb(S, "g_qeT", [128, 4, 128], BF16)
                B_.kbT = sb(S, "g_kbT", [128, 4, 128], BF16)
                B_.aTm = sb(S, "g_aTm", [128, 4, 128], BF16)
                B_.osb = B_.xt
                B_.ssq = sb(S, "g_ssq", [128, 4], F32)
                B_.rsq = sb(S, "g_rsq", [128, 4], F32)
                B_.sg = sb(S, "g_sg", [128, 1024], F32)
                B_.ob = B_.xn
                B_.obT = sb(S, "g_obT", [128, 8, 128], BF16)
                gsets.append(B_)

            def gla_tile(B_, hT, c0, nt, full, mix_col, sidx, xrow=None):
                fbT, t1, spl, Ed, kd, vsb, eb = B_.fbT, B_.t1, B_.spl, B_.Ed, B_.kd, B_.vsb, B_.eb
                Epos, Eneg, qeT, kbT, aTm, osb, ssq, rsq, sg, ob, obT = (B_.Epos, B_.Eneg, B_.qeT, B_.kbT, B_.aTm, B_.osb, B_.ssq,
                                                                         B_.rsq, B_.sg, B_.ob, B_.obT)
                Sb_in = Sbfs[sidx]
                Sb_out = Sbfs[(sidx + 1) % NSB]
                if xrow is not None:
                    dma(lambda e: e.dma_start(out=B_.xt[0:nt, :], in_=xp[xrow:xrow + nt, :]), w=[B_.xt])
                    yield
                    norm_part1(B_.xt, nt, B_.xn, B_.ss, B_.rstd, junkD)
                    yield
                    norm_part2(nt, "norm_mix_pre", B_.xn, hT, c0)
                    dma(lambda e: e.dma_start(out=hTv[:, :, xrow:xrow + nt], in_=hT[:, :, c0:c0 + nt]), r=[hT])
                    yield
                p = getS()
                for k in range(8):
                    mm(p[0:16, 0:nt], Wfb[:, k, :], hT[:, k, c0:c0 + nt], [Wfb, hT], [p], start=(k == 0), stop=(k == 7))
                act(lambda e, p=p: e.copy(out=fbT[0:16, 0:nt], in_=p[0:16, 0:nt]), r=[p], w=[fbT])
                for half in range(2):
                    p = getF()
                    for k in range(8):
                        mm(p[0:nt, :], hT[:, k, c0:c0 + nt], Wv[:, k, half * 512:(half + 1) * 512], [hT, Wv], [p],
                           start=(k == 0), stop=(k == 7))
                    act(lambda e, p=p, half=half: e.copy(out=vsb[0:nt, half * 512:(half + 1) * 512], in_=p[0:nt, :]), r=[p], w=[vsb])
                yield
                p = getF()
                mm(p[0:nt, :], fbT[0:17, 0:nt], wf2_sb[:], [fbT, wf2_sb], [p])
                act(lambda e, p=p: e.activation(out=t1[0:nt, :], in_=p[0:nt, :], func=AF.Exp, scale=-1.0), r=[p], w=[t1])
                act(lambda e: e.activation(out=spl[0:nt, :], in_=t1[0:nt, :], func=AF.Ln, bias=1.0), r=[t1], w=[spl])
                yield
                pkt = getF()
                for k in range(8):
                    mm(pkt[0:nt, :], hT[:, k, c0:c0 + nt], Wk[:, k, :], [hT, Wk], [pkt], start=(k == 0), stop=(k == 7))
                if full:
                    pq = getF()
                    pk = getF()
                    for h in range(4):
                        for k in range(8):
                            mm(pq[:, h * nt:(h + 1) * nt], Wq[:, k, h * 128:(h + 1) * 128], hT[:, k, c0:c0 + nt], [Wq, hT], [pq],
                               start=(k == 0), stop=(k == 7))
                        for k in range(8):
                            mm(pk[:, h * nt:(h + 1) * nt], Wk[:, k, h * 128:(h + 1) * 128], hT[:, k, c0:c0 + nt], [Wk, hT], [pk],
                               start=(k == 0), stop=(k == 7))
                p = getF()
                mm(p[0:nt, :], triL[0:nt, 0:nt], spl[0:nt, :], [triL, spl], [p])
                act(lambda e, p=p: e.activation(out=Ed[0:nt, :], in_=p[0:nt, :], func=AF.Exp), r=[p], w=[Ed])
                dve(lambda e, pkt=pkt: e.tensor_tensor(out=kd[0:nt, :], in0=pkt[0:nt, :], in1=Ed[0:nt, :], op=ALU.mult), r=[pkt, Ed], w=[kd])
                if not full:
                    p = getS()
                    for h in range(4):
                        mm(p[:, h:h + 1], spl[0:nt, h * 128:(h + 1) * 128], m16[0:nt, 0:1], [spl, m16], [p])
                    act(lambda e, p=p: e.activation(out=eb[:], in_=p[:, 0:4], func=AF.Exp), r=[p], w=[eb])
                if full:
                    pb_ = getF()
                    for h in range(4):
                        mm(pb_[:, h * nt:(h + 1) * nt], spl[0:nt, h * 128:(h + 1) * 128], triU[0:nt, 0:nt], [spl, triU], [pb_])
                    act(lambda e, pb_=pb_: e.activation(out=eb[:], in_=pb_[:, 0:4 * nt].rearrange("p (h t) -> p h t", h=4)[:, :, nt - 1],
                                                        func=AF.Exp), r=[pb_], w=[eb])
                    act(lambda e, pb_=pb_: e.activation(out=Epos[:, 0:4 * nt], in_=pb_[:, 0:4 * nt], func=AF.Exp), r=[pb_], w=[Epos])
                    act(lambda e, pb_=pb_: e.activation(out=Eneg[:, 0:4 * nt], in_=pb_[:, 0:4 * nt], func=AF.Exp, scale=-1.0), r=[pb_], w=[Eneg])
                    dve(lambda e, pq=pq: e.scalar_tensor_tensor(out=qeT[:, :, 0:nt], in0=pq[:, 0:4 * nt].rearrange("p (h t) -> p h t", h=4),
                                                                scalar=SC_B, in1=Epos[:, 0:4 * nt].rearrange("p (h t) -> p h t", h=4),
                                                                op0=ALU.mult, op1=ALU.mult), r=[pq, Epos], w=[qeT])
                    dve(lambda e, pk=pk: e.tensor_tensor(out=kbT[:, :, 0:nt], in0=pk[:, 0:4 * nt].rearrange("p (h t) -> p h t", h=4),
                                                         in1=Eneg[:, 0:4 * nt].rearrange("p (h t) -> p h t", h=4), op=ALU.mult),
                        r=[pk, Eneg], w=[kbT])
                yield
                for hp in range(2):
                    p = getF()
                    for hh in range(2):
                        h = hp * 2 + hh
                        mm(p[:, hh * 256:(hh + 1) * 256], kd[0:nt, h * 128:(h + 1) * 128], vsb[0:nt, h * 256:(h + 1) * 256], [kd, vsb], [p])
                    for hh in range(2):
                        h = hp * 2 + hh
                        dve(lambda e, p=p, h=h, hh=hh: e.scalar_tensor_tensor(out=Sst[:, h, :], in0=Sst[:, h, :], scalar=eb[:, h:h + 1],
                                                                              in1=p[:, hh * 256:(hh + 1) * 256], op0=ALU.mult, op1=ALU.add),
                            r=[Sst, eb, p], w=[Sst])
                pool(lambda e: e.tensor_copy(out=Sb_out[:], in_=Sst[:]), r=[Sst], w=[Sb_out])
                if not full:
                    return
                yield
                pa = getF()
                for h in range(4):
                    mm(pa[0:nt, h * nt:(h + 1) * nt], kbT[:, h, 0:nt], qeT[:, h, 0:nt], [kbT, qeT], [pa])
                dve(lambda e, pa=pa: e.tensor_tensor(out=aTm[0:nt, :, 0:nt], in0=pa[0:nt, 0:4 * nt].rearrange("p (h t) -> p h t", h=4),
                                                     in1=Uf[0:nt, 0:nt].unsqueeze(1).broadcast_to([nt, 4, nt]), op=ALU.mult),
                    r=[pa, Uf], w=[aTm])
                for half in range(2):
                    p = getF()
                    for k in range(8):
                        mm(p[0:nt, :], hT[:, k, c0:c0 + nt], Wg[:, k, half * 512:(half + 1) * 512], [hT, Wg], [p],
                           start=(k == 0), stop=(k == 7))
                    act(lambda e, p=p, half=half: e.activation(out=sg[0:nt, half * 512:(half + 1) * 512], in_=p[0:nt, :], func=AF.Silu),
                        r=[p], w=[sg])
                pool(lambda e: e.tensor_tensor(out=sg[0:nt, :].rearrange("p (h e) -> p h e", h=4),
                                               in0=sg[0:nt, :].rearrange("p (h e) -> p h e", h=4),
                                               in1=gn_sb[0:nt, :].unsqueeze(1).broadcast_to([nt, 4, 256]), op=ALU.mult),
                     r=[sg, gn_sb], w=[sg])
                yield
                for hp in range(2):
                    p = getF()
                    for hh in range(2):
                        h = hp * 2 + hh
                        mm(p[0:nt, hh * 256:(hh + 1) * 256], qeT[:, h, 0:nt], Sb_in[:, h, :], [qeT, Sb_in], [p], start=True, stop=False)
                        mm(p[0:nt, hh * 256:(hh + 1) * 256], aTm[0:nt, h, 0:nt], vsb[0:nt, h * 256:(h + 1) * 256], [aTm, vsb], [p],
                           start=False, stop=True)
                    act(lambda e, p=p, hp=hp: e.copy(out=osb[0:nt, hp * 512:(hp + 1) * 512], in_=p[0:nt, :]), r=[p], w=[osb])
                for h in range(4):
                    act(lambda e, h=h: e.activation(out=junk[0:nt, :], in_=osb[0:nt, h * 256:(h + 1) * 256], func=AF.Square,
                                                    accum_out=ssq[0:nt, h:h + 1]), r=[osb, junk], w=[junk, ssq])
                act(lambda e: e.activation(out=rsq[0:nt, :], in_=ssq[0:nt, :], func=AF.Ln, scale=1.0 / 256, bias=EPS), r=[ssq], w=[rsq])
                act(lambda e: e.activation(out=rsq[0:nt, :], in_=rsq[0:nt, :], func=AF.Exp, scale=-0.5), r=[rsq], w=[rsq])
                yield
                for h in range(4):
                    dve(lambda e, h=h: e.scalar_tensor_tensor(out=ob[0:nt, h * 256:(h + 1) * 256], in0=osb[0:nt, h * 256:(h + 1) * 256],
                                                              scalar=rsq[0:nt, h:h + 1], in1=sg[0:nt, h * 256:(h + 1) * 256],
                                                              op0=ALU.mult, op1=ALU.mult), r=[osb, rsq, sg], w=[ob])
                yield
                pt = getB()
                for c in range(8):
                    pe(lambda e, c=c, pt=pt: e.transpose(out=pt[:, c * nt:(c + 1) * nt], in_=ob[0:nt, c * 128:(c + 1) * 128],
                                                          identity=ident_b[0:nt, 0:nt]), r=[ob, ident_b], w=[pt])
                act(lambda e, pt=pt: e.copy(out=obT[:, :, 0:nt], in_=pt[:, 0:8 * nt].rearrange("p (c t) -> p c t", c=8)), r=[pt], w=[obT])
                mc = mix_col
                dma(lambda e: e.dma_start(out=obTv[:, :, mc:mc + nt], in_=obT[:, :, 0:nt]), r=[obT])

            pool(lambda e: e.memset(Sst[:], 0.0), w=[Sst])
            pool(lambda e: e.memset(Sbfs[0][:], 0.0), w=[Sbfs[0]])
            NG = (NPRE + NOWN) // 512

            def gla_gens():
                tcount = 0
                for grp in range(0 if DBG_SKIP12 else NG):
                    for s_ in range(4):
                        tok = grp * 512 + s_ * 128
                        full = tok >= NPRE
                        B_ = gsets[tcount % NGS]
                        yield gla_tile(B_, B_.hT, 0, 128, full, tok - NPRE, tcount % NSB, tok)
                        tcount += 1
            run_pipeline(gla_gens(), 4, 3)
            nt_total = 0 if DBG_SKIP12 else NG * 4
            dma(lambda e: e.dma_start(out=glap_o.rearrange("h d e -> d h e"), in_=Sst[:]), r=[Sst])
            hT = hsm
            B0 = gsets[0]
            dma(lambda e: e.dma_start(out=B0.xt[0:NS, :], in_=xs), w=[B0.xt])
            norm_transpose(B0.xt, NS, "norm_mix_pre", B0.xn, B0.ss, B0.rstd, junkD, hsm, 0)
            dma(lambda e: e.dma_start(out=hTv[:, :, SOFF:SOFF + NS], in_=hsm[:]), r=[hsm])
            for b in range(0 if DBG_SKIP12 else 4):
                dma(lambda e, b=b: e.dma_start(out=Sst[:], in_=sgla[b].rearrange("h d e -> d h e")), w=[Sst])
                act(lambda e: e.copy(out=Sbfs[0][:], in_=Sst[:]), r=[Sst], w=[Sbfs[0]])
                for _ in gla_tile(gsets[b % 2], hT, b * 8, 8, True, NOWN + b * 8, 0):
                    pass
                dma(lambda e, b=b: e.dma_start(out=glas_o[b].rearrange("h d e -> d h e"), in_=Sst[:]), r=[Sst])
            P.barrier()

        with contextlib.ExitStack() as S:
            Wga = sb(S, "h_Wga", [128, 8, 1024], BF16)
            Wgb = sb(S, "h_Wgb", [128, 8, 1024], BF16)
            Wpb = sb(S, "h_Wpb", [128, 8, 1024], BF16)
            dma(lambda e: e.dma_start(out=Wgb[:], in_=wview("w_in")[:, :, GBT:GBT + 1024]), w=[Wgb], q="pool")
            dma(lambda e: e.dma_start(out=Wpb[:], in_=wview("w_proj_b")), w=[Wpb], q="pool")
            dma(lambda e: e.dma_start(out=Wga[:], in_=wview("w_in")[:, :, GA:GA + 1024]), w=[Wga], q="pool")
            hTg = [sb(S, "h_hT%d" % i, [128, 8, 512], BF16) for i in range(2)]
            obg = [sb(S, "h_ob%d" % i, [128, 8, 512], BF16) for i in range(2)]
            sgt2 = [sb(S, "h_sg%d" % i, [128, 512], F32) for i in range(2)]
            partg = [sb(S, "h_part%d" % i, [128, 8, 512], F32) for i in range(2)]
            sgag = [sb(S, "h_sga%d" % i, [128, 8, 512], BF16) for i in range(2)]

            def gate_group(gi):
                smp = gi == NOWN // 512
                ntok = NS if smp else 512
                hcol = SOFF if smp else NPRE + gi * 512
                mcol = gi * 512
                hT, obt, pg_, sa_ = hTg[gi % 2], obg[gi % 2], partg[gi % 2], sgag[gi % 2]
                dma(lambda e: e.dma_start(out=hT[:, :, 0:ntok], in_=hTv[:, :, hcol:hcol + ntok]), w=[hT])
                dma(lambda e: e.dma_start(out=obt[:, :, 0:ntok], in_=obTv[:, :, mcol:mcol + ntok]), w=[obt])
                yield
                for c in range(8):
                    if c == 4:
                        yield
                    p = getF()
                    for k in range(8):
                        mm(p[:, 0:ntok], Wgb[:, k, c * 128:(c + 1) * 128], hT[:, k, 0:ntok], [Wgb, hT], [p], start=(k == 0), stop=(k == 7))
                    sg_ = sgt2[c % 2]
                    act(lambda e, p=p, sg_=sg_: e.activation(out=sg_[:, 0:ntok], in_=p[:, 0:ntok], func=AF.Sigmoid), r=[p], w=[sg_])
                    p = getF()
                    for k in range(8):
                        mm(p[:, 0:ntok], Wpb[:, k, c * 128:(c + 1) * 128], obt[:, k, 0:ntok], [Wpb, obt], [p], start=(k == 0), stop=(k == 7))
                    dve(lambda e, p=p, sg_=sg_, c=c: e.tensor_tensor(out=pg_[:, c, 0:ntok], in0=p[:, 0:ntok], in1=sg_[:, 0:ntok], op=ALU.mult),
                        r=[p, sg_], w=[pg_])
                    p = getF()
                    for k in range(8):
                        mm(p[:, 0:ntok], Wga[:, k, c * 128:(c + 1) * 128], hT[:, k, 0:ntok], [Wga, hT], [p], start=(k == 0), stop=(k == 7))
                    act(lambda e, p=p, c=c: e.activation(out=sa_[:, c, 0:ntok], in_=p[:, 0:ntok], func=AF.Sigmoid), r=[p], w=[sa_])
                dma(lambda e: e.dma_start(out=mixpv[:, :, mcol:mcol + ntok], in_=pg_[:, :, 0:ntok]), r=[pg_], q="pool")
                dma(lambda e: e.dma_start(out=sgav[:, :, mcol:mcol + ntok], in_=sa_[:, :, 0:ntok]), r=[sa_], q="pool")

            run_pipeline([gate_group(gi) for gi in range(0 if DBG_SKIP12 else NOWN // 512 + 1)], 2, 1)
            P.barrier()

        if STOP == 2:
            GE.close()
            P.emit()
            return nc
        def attn_group(g, W, dil, S):
            Wq = sb(S, "a_Wq", [128, 8, 512], BF16)
            Wk = sb(S, "a_Wk", [128, 8, 512], BF16)
            Wv = sb(S, "a_Wv", [128, 8, 512], BF16)
            wv_in = wview("w_in")
            for (wt, c0) in ((Wk, KA + 512 * g), (Wv, VA + 512 * g), (Wq, QA + 512 * g)):
                dma(lambda e, wt=wt, c0=c0: e.dma_start(out=wt[:], in_=wv_in[:, :, c0:c0 + 512]), w=[wt], q="pool")
            hT = sb(S, "a_hT", [128, 8, 2048], BF16)
            hTb = [Buf("hTq%d" % i) for i in range(4)]
            QT = [sb(S, "a_QT%d" % h, [128, 2048], BF16) for h in range(4)]
            KT = [[sb(S, "a_KT%d_%d" % (i, h), [128, 2048], BF16) for h in range(4)] for i in range(2)]
            VA_ = [sb(S, "a_V%d" % i, [128, 16, 4, 130], BF16) for i in range(2)]
            stg = [sb(S, "a_stg%d" % i, [128, 512], F32) for i in range(2)]
            NBS = 2
            expS = [sb(S, "a_expS%d" % i, [128, 4, 256], F32) for i in range(NBS)]
            Pm = [sb(S, "a_P%d" % i, [128, 4, 256], BF16) for i in range(NBS)]
            Osb = [sb(S, "a_O%d" % i, [128, 4, 129], F32) for i in range(NBS)]
            for i in range(2):
                pool(lambda e, i=i: e.memset(VA_[i][:, :, :, 128:130], 1.0), w=[VA_[i]])
            ncls = dil
            nper = 2048 // dil
            nbc = nper // 128
            rows = (NPRE - 2048, NPRE, NPRE + 2048)

            def hquarters(sti):
                return (3,) if (sti == 0 and dil <= 4) else (0, 1, 2, 3)

            def hload(sti):
                row0 = rows[sti]
                for q4 in hquarters(sti):
                    dma(lambda e, q4=q4: e.dma_start(out=hT[:, :, q4 * 512:(q4 + 1) * 512],
                                                     in_=hTv[:, :, row0 + q4 * 512:row0 + (q4 + 1) * 512]), w=[hTb[q4]])

            hs = sb(S, "s_hT", [128, 8, NS], BF16)
            qTs = sb(S, "s_qT", [128, 4, NS], BF16)
            kTs = sb(S, "s_kT", [128, 4, NS], BF16)
            kc = [sb(S, "s_kc%d" % i, [128, 512], F32) for i in range(2)]
            vc = [sb(S, "s_vc%d" % i, [128, 512], F32) for i in range(2)]
            kcT = [sb(S, "s_kcT%d" % i, [128, 4, 128], BF16) for i in range(2)]
            vca = [sb(S, "s_vca%d" % i, [128, 4, 130], BF16) for i in range(2)]
            vna = [sb(S, "s_vna%d" % i, [8, 4, 130], BF16) for i in range(2)]
            kvst = sb(S, "s_kvst", [NS, 2, 512], F32)
            vall = sb(S, "s_vall", [NS, 4, 130], BF16)
            ex1 = [sb(S, "s_ex1%d" % i, [128, 4, 8], F32) for i in range(2)]
            ex2 = [sb(S, "s_ex2%d" % i, [8, 4, 8], F32) for i in range(2)]
            p1s = [sb(S, "s_p1%d" % i, [128, 4, 8], BF16) for i in range(2)]
            p2s = [sb(S, "s_p2%d" % i, [8, 4, 8], BF16) for i in range(2)]
            os_ = [sb(S, "s_o%d" % i, [8, 4, 129], F32) for i in range(2)]
            for i in range(2):
                pool(lambda e, i=i: e.memset(vca[i][:, :, 128:130], 1.0), w=[vca[i]])
            nq = 8 // dil if dil <= 8 else 1
            ncl = min(dil, 8)
            hload(0)
            dma(lambda e: e.dma_start(out=hs[:], in_=hTv[:, :, SOFF:SOFF + NS]), w=[hs])
            for (dst, Wt) in ((qTs, Wq), (kTs, Wk)):
                p = getF()
                for h in range(4):
                    for k in range(8):
                        mm(p[:, h * NS:(h + 1) * NS], Wt[:, k, h * 128:(h + 1) * 128], hs[:, k, :], [Wt, hs], [p], start=(k == 0), stop=(k == 7))
                act(lambda e, p=p, dst=dst: e.copy(out=dst[:], in_=p[:, 0:4 * NS].rearrange("p (h t) -> p h t", h=4)), r=[p], w=[dst])

            pool(lambda e: e.memset(vall[:, :, 128:130], 1.0), w=[vall])
            for kv, Wt in ((0, Wk), (1, Wv)):
                p = getF()
                for k in range(8):
                    mm(p[0:NS, :], hs[:, k, :], Wt[:, k, :], [hs, Wt], [p], start=(k == 0), stop=(k == 7))
                act(lambda e, p=p, kv=kv: e.copy(out=kvst[:, kv, :], in_=p[0:NS, :]), r=[p], w=[kvst])
            pool(lambda e: e.tensor_copy(out=vall[:, :, 0:128], in_=kvst[:, 1, :].rearrange("p (h d) -> p h d", h=4)), r=[kvst], w=[vall])
            dma(lambda e: e.dma_start(out=ws_o[g], in_=kvst[:]), r=[kvst])

            def sample_unit(b, r, i2):
                tcol = b * 8 + r
                tsl = slice(tcol, tcol + dil * (nq - 1) + 1, dil)
                dma(lambda e: e.dma_start(out=kc[i2][:], in_=cw[g][b, r:r + dil * 127 + 1:dil, 0, :]), w=[kc[i2]])
                dma(lambda e: e.dma_start(out=vc[i2][:], in_=cw[g][b, r:r + dil * 127 + 1:dil, 1, :]), w=[vc[i2]])
                yield
                p = getF()
                for h in range(4):
                    pe(lambda e, p=p, h=h: e.transpose(out=p[:, h * 128:(h + 1) * 128], in_=kc[i2][:, h * 128:(h + 1) * 128],
                                                       identity=ident_f[:]), r=[kc[i2], ident_f], w=[p])
                act(lambda e, p=p: e.copy(out=kcT[i2][:], in_=p[:, :].rearrange("p (h n) -> p h n", h=4)), r=[p], w=[kcT[i2]])
                pool(lambda e: e.tensor_copy(out=vca[i2][:, :, 0:128], in_=vc[i2][:].rearrange("p (h d) -> p h d", h=4)),
                     r=[vc[i2]], w=[vca[i2]])
                dma(lambda e: e.dma_start(out=vna[i2][0:nq, :, :], in_=vall[tsl, :, :]), r=[vall], w=[vna[i2]])
                yield
                pa = getS()
                pb_ = getS()
                for h in range(4):
                    mm(pa[:, h * 8:h * 8 + nq], kcT[i2][:, h, :], qTs[:, h, tsl], [kcT[i2], qTs], [pa])
                    mm(pb_[0:nq, h * 8:h * 8 + nq], kTs[:, h, tsl], qTs[:, h, tsl], [kTs, qTs], [pb_])
                act(lambda e, pa=pa: e.activation(out=ex1[i2][:, :, 0:nq], in_=pa[:, 0:32].rearrange("p (h q) -> p h q", h=4)[:, :, 0:nq],
                                                  func=AF.Exp, scale=SC_A), r=[pa], w=[ex1[i2]])
                act(lambda e, pb_=pb_: e.activation(out=ex2[i2][0:nq, :, 0:nq],
                                                    in_=pb_[0:nq, 0:32].rearrange("p (h q) -> p h q", h=4)[:, :, 0:nq],
                                                    func=AF.Exp, scale=SC_A), r=[pb_], w=[ex2[i2]])
                yield
                pool(lambda e: e.tensor_tensor(out=p1s[i2][:, :, 0:nq], in0=ex1[i2][:, :, 0:nq], in1=Eg[g][:, :, 128:128 + nq], op=ALU.mult),
                     r=[ex1[i2], Eg[g]], w=[p1s[i2]])
                pool(lambda e: e.tensor_tensor(out=p2s[i2][0:nq, :, 0:nq], in0=ex2[i2][0:nq, :, 0:nq], in1=Eg[g][0:nq, :, 0:nq], op=ALU.mult),
                     r=[ex2[i2], Eg[g]], w=[p2s[i2]])
                yield
                for hp in range(2):
                    p = getF()
                    for hh in range(2):
                        h = hp * 2 + hh
                        mm(p[0:nq, hh * 256:hh * 256 + 129], p1s[i2][:, h, 0:nq], vca[i2][:, h, 0:129], [p1s[i2], vca[i2]], [p],
                           start=True, stop=False)
                        mm(p[0:nq, hh * 256:hh * 256 + 129], p2s[i2][0:nq, h, 0:nq], vna[i2][0:nq, h, 0:129], [p2s[i2], vna[i2]], [p],
                           start=False, stop=True)
                    act(lambda e, p=p, hp=hp: e.copy(out=os_[i2][0:nq, hp * 2:hp * 2 + 2, :],
                                                     in_=p[0:nq, :].rearrange("p (h c) -> p h c", h=2)[:, :, 0:129]), r=[p], w=[os_[i2]])
                t0 = NOWN + tcol
                dma(lambda e: e.dma_start(out=accd[g, t0:t0 + dil * (nq - 1) + 1:dil, :],
                                          in_=os_[i2][0:nq].rearrange("p h c -> p (h c)")), r=[os_[i2]])

            units = [(b, r) for b in range(4 if DBG_SAMPLE else 0) for r in range(ncl)]
            ucount = [0]

            def next_unit():
                if not units:
                    return None
                b, r = units.pop(0)
                i2 = ucount[0] % 2
                ucount[0] += 1
                return sample_unit(b, r, i2)

            def attn_block(sti, cur, prv, row0, r, nb, bi):
                blk = r * nbc + nb
                cs = r * nper + nb * 128
                if nb > 0:
                    pKT, pV, pblk, pcs = KT[cur], VA_[cur], blk - 1, cs - 128
                else:
                    pKT, pV, pblk, pcs = KT[prv], VA_[prv], r * nbc + nbc - 1, r * nper + (nbc - 1) * 128
                Et = Egf[g] if (sti == 1 and nb == 0) else Eg[g]
                ex = expS[bi % NBS]
                pm_ = Pm[bi % NBS]
                ot = Osb[bi % NBS]
                for hp in range(2):
                    p = getF()
                    for hh in range(2):
                        h = hp * 2 + hh
                        mm(p[:, hh * 256:hh * 256 + 128], KT[cur][h][:, cs:cs + 128], QT[h][:, cs:cs + 128], [KT[cur][h], QT[h]], [p])
                        mm(p[:, hh * 256 + 128:hh * 256 + 256], pKT[h][:, pcs:pcs + 128], QT[h][:, cs:cs + 128], [pKT[h], QT[h]], [p])
                    act(lambda e, p=p, hp=hp: e.activation(out=ex[:, hp * 2:hp * 2 + 2, :],
                                                           in_=p[:, :].rearrange("p (h c) -> p h c", h=2), func=AF.Exp, scale=SC_A),
                        r=[p], w=[ex])
                yield
                eng_ = dve if bi % 3 != 2 else pool
                eng_(lambda e: e.tensor_tensor(out=pm_[:], in0=ex[:], in1=Et[:], op=ALU.mult), r=[ex, Et], w=[pm_])
                yield
                for hp in range(2):
                    p = getF()
                    for hh in range(2):
                        h = hp * 2 + hh
                        mm(p[:, hh * 256:hh * 256 + 129], pm_[:, h, 0:128], VA_[cur][:, blk, h, 0:129], [pm_, VA_[cur]], [p],
                           start=True, stop=False)
                        mm(p[:, hh * 256:hh * 256 + 129], pm_[:, h, 128:256], pV[:, pblk, h, 0:129], [pm_, pV], [p],
                           start=False, stop=True)
                    act(lambda e, p=p, hp=hp: e.copy(out=ot[:, hp * 2:hp * 2 + 2, :],
                                                     in_=p[:, :].rearrange("p (h c) -> p h c", h=2)[:, :, 0:129]), r=[p], w=[ot])
                tok0 = (row0 - NPRE) + r + dil * 128 * nb
                dma(lambda e: e.dma_start(out=accd[g, tok0:tok0 + dil * 127 + 1:dil, :],
                                          in_=ot[:].rearrange("p h c -> p (h c)")), r=[ot])

            stgc = 0
            bcount = 0
            for sti, row0 in enumerate(rows if DBG_PROMPT else ()):
                own = sti > 0
                cur = sti % 2
                prv = 1 - cur
                hq = hquarters(sti)
                for (dstl, Wt, need) in ((KT[cur], Wk, True), (QT, Wq, own)):
                    if not need:
                        continue
                    for h in range(4):
                        dst = dstl[h]
                        for s_ in hq:
                            p = getF()
                            for k in range(8):
                                mm(p[:, :], Wt[:, k, h * 128:(h + 1) * 128], hT[:, k, s_ * 512:(s_ + 1) * 512], [Wt, hTb[s_]], [p],
                                   start=(k == 0), stop=(k == 7))
                            npc = 512 // dil
                            o_ap = dst[:, :].rearrange("p (r n) -> p r n", r=dil)[:, :, npc * s_:npc * (s_ + 1)]
                            i_ap = p[:, :].rearrange("p (m r) -> p r m", r=dil)
                            if h % 2 == 0:
                                act(lambda e, o_ap=o_ap, i_ap=i_ap: e.copy(out=o_ap, in_=i_ap), r=[p], w=[dst])
                            else:
                                dve(lambda e, o_ap=o_ap, i_ap=i_ap: e.tensor_copy(out=o_ap, in_=i_ap), r=[p], w=[dst])
                for r in range(ncls):
                    for nb in range(nbc):
                        if sti == 0 and nb != nbc - 1:
                            continue
                        blk = r * nbc + nb
                        u0 = r + dil * 128 * nb
                        qs = sorted(set([u0 // 512, (u0 + dil * 127) // 512])) if dil <= 4 else [0, 1, 2, 3]
                        hdeps = [hTb[q] for q in range(qs[0], qs[-1] + 1)]
                        p = getF()
                        for k in range(8):
                            mm(p[:, :], hT[:, k, u0:u0 + dil * 127 + 1:dil], Wv[:, k, :], hdeps + [Wv], [p], start=(k == 0), stop=(k == 7))
                        act(lambda e, p=p, blk=blk: e.copy(out=VA_[cur][:, blk, :, 0:128],
                                                           in_=p[:, :].rearrange("p (h d) -> p h d", h=4)), r=[p], w=[VA_[cur]])
                        tok0 = (row0 - NPRE) + u0
                        if sti == 2 and tok0 >= NOWN - W and DBG_W:
                            orow = tok0 - (NOWN - W)
                            s1 = stg[stgc % 2]
                            stgc += 1
                            act(lambda e, p=p, s1=s1: e.copy(out=s1[:], in_=p[:, :]), r=[p], w=[s1])
                            dma(lambda e, s1=s1, orow=orow: e.dma_start(out=wp_o[g][orow:orow + dil * 127 + 1:dil, 1, :], in_=s1[:]), r=[s1])
                            p2 = getF()
                            for k in range(8):
                                mm(p2[:, :], hT[:, k, u0:u0 + dil * 127 + 1:dil], Wk[:, k, :], hdeps + [Wk], [p2], start=(k == 0), stop=(k == 7))
                            s2 = stg[stgc % 2]
                            stgc += 1
                            act(lambda e, p2=p2, s2=s2: e.copy(out=s2[:], in_=p2[:, :]), r=[p2], w=[s2])
                            dma(lambda e, s2=s2, orow=orow: e.dma_start(out=wp_o[g][orow:orow + dil * 127 + 1:dil, 0, :], in_=s2[:]), r=[s2])
                if sti + 1 < len(rows):
                    hload(sti + 1)
                if not own:
                    continue
                gens = []
                for r in range(ncls):
                    for nb in range(nbc):
                        gens.append(attn_block(sti, cur, prv, row0, r, nb, bcount))
                        bcount += 1
                        u_ = next_unit()
                        if u_ is not None:
                            gens.append(u_)
                run_pipeline(gens, 3, 1)
            rest = []
            while True:
                u_ = next_unit()
                if u_ is None:
                    break
                rest.append(u_)
            run_pipeline(rest, 2, 2)
            P.barrier()

        for g, (W, dil) in enumerate(GROUPS):
            if g not in DBG_GROUPS:
                continue
            with contextlib.ExitStack() as S:
                attn_group(g, W, dil, S)

        if STOP == 3:
            GE.close()
            P.emit()
            return nc
        GE.close()
        with contextlib.ExitStack() as S:
            Wpa = sb(S, "c_Wpa", [128, 4, 1024], BF16)
            Wout = sb(S, "c_Wout", [128, 8, 1024], BF16)
            Wmq = sb(S, "c_Wmq", [128, 8, 1024], BF16)
            Wmo = sb(S, "c_Wmo", [128, 8, 1024], BF16)
            gpost = {n: sb(S, "c_g_" + n, [128, D], F32) for n in ("norm_mix_post", "norm_mem_post")}
            for n in gpost:
                dma(lambda e, n=n: e.dma_start(out=gpost[n][:], in_=nrm[n][0:1, :].broadcast_to([128, D])), w=[gpost[n]])
            KmT = sb(S, "c_KmT", [128, 8, 256], BF16)
            Vm = sb(S, "c_Vm", [128, 2, 1024], BF16)
            junk = sb(S, "c_junk", [128, D], BF16)
            class Set4:
                pass
            sets = []
            S3rd = contextlib.ExitStack()

            def mkset(S_):
                B_ = Set4()
                big = sb(S_, "c_big", [128, 1548], F32)
                B_.acc = View(big, lambda big=big: big[:, :].rearrange("p (g f) -> p g f", g=3))
                B_.msb = View(big, lambda big=big: big[:, 0:1024])
                B_.num = sb(S_, "c_num", [128, 4, 129], F32)
                B_.rden = sb(S_, "c_rden", [128, 4], F32)
                B_.comb = sb(S_, "c_comb", [128, 512], BF16)
                B_.combT = sb(S_, "c_combT", [128, 4, 128], BF16)
                B_.sga = sb(S_, "c_sga", [128, 8, 128], BF16)
                B_.x1 = sb(S_, "c_x1", [128, D], F32)
                B_.part = View(B_.x1, lambda x1=B_.x1: x1[:, :].rearrange("p (c t) -> p c t", c=8))
                B_.mixT = sb(S_, "c_mixT", [128, 8, 128], BF16)
                B_.h2T = B_.mixT
                B_.onT = B_.mixT
                B_.qmT = sb(S_, "c_qmT", [128, 8, 128], BF16)
                B_.Pmm = sb(S_, "c_Pmm", [128, 8, 128], BF16)
                B_.dens = sb(S_, "c_dens", [128, 4], F32)
                B_.xt = sb(S_, "c_xt", [128, D], F32)
                B_.xn = sb(S_, "c_xn2", [128, D], BF16)
                B_.onb = B_.xn
                B_.ss = sb(S_, "c_ss2", [128, 1], F32)
                B_.rstd = sb(S_, "c_rstd2", [128, 1], F32)
                sets.append(B_)
            mkset(S)
            mkset(S)

            with contextlib.ExitStack() as S2:
                Wmk = sb(S2, "c_Wmk", [128, 8, 1024], BF16)
                Wmv = sb(S2, "c_Wmv", [128, 8, 1024], BF16)
                dma(lambda e: e.dma_start(out=Wmk[:], in_=wview("w_mk")), w=[Wmk], q="pool")
                dma(lambda e: e.dma_start(out=Wmv[:], in_=wview("w_mv")), w=[Wmv], q="pool")
                dma(lambda e: e.dma_start(out=Wpa[:], in_=wview("w_proj_a")), w=[Wpa], q="pool")
                dma(lambda e: e.dma_start(out=Wout[:], in_=wview("w_out")), w=[Wout], q="pool")
                dma(lambda e: e.dma_start(out=Wmq[:], in_=wview("w_mq")), w=[Wmq], q="pool")
                dma(lambda e: e.dma_start(out=Wmo[:], in_=wview("w_mo")), w=[Wmo], q="pool")
                mT = sb(S2, "c_mT", [128, 8, 256], BF16)
                kvs = sb(S2, "c_kvs", [128, 2, 1024], F32)
                for mb in range(2):
                    Bm = sets[mb]
                    dma(lambda e, mb=mb, Bm=Bm: e.dma_start(out=Bm.xt[:], in_=mem[mb * 128:(mb + 1) * 128, :]), w=[Bm.xt])
                    norm_transpose(Bm.xt, 128, "norm_memtok", Bm.xn, Bm.ss, Bm.rstd, junk, mT, mb * 128)
                for mb in range(2):
                    for kv, Wt in ((0, Wmk), (1, Wmv)):
                        for half in range(2):
                            p = getF()
                            for k in range(8):
                                mm(p[:, :], mT[:, k, mb * 128:(mb + 1) * 128], Wt[:, k, half * 512:(half + 1) * 512], [mT, Wt], [p],
                                   start=(k == 0), stop=(k == 7))
                            act(lambda e, p=p, kv=kv, half=half: e.copy(out=kvs[:, kv, half * 512:(half + 1) * 512], in_=p[:, :]), r=[p], w=[kvs])
                            if kv == 1:
                                dve(lambda e, mb=mb, half=half: e.tensor_copy(out=Vm[:, mb, half * 512:(half + 1) * 512], in_=kvs[:, 1, half * 512:(half + 1) * 512]),
                                    r=[kvs], w=[Vm])
                    dma(lambda e, mb=mb: e.dma_start(out=memkv_o[mb * 128:(mb + 1) * 128, :, :], in_=kvs[:]), r=[kvs])
                for c in range(8):
                    p = getF()
                    for k in range(8):
                        mm(p[:, 0:256], Wmk[:, k, c * 128:(c + 1) * 128], mT[:, k, :], [Wmk, mT], [p], start=(k == 0), stop=(k == 7))
                    act(lambda e, p=p, c=c: e.copy(out=KmT[:, c, :], in_=p[:, 0:256]), r=[p], w=[KmT])
                P.barrier()
            mkset(S3rd)
            mkset(S3rd)
            def post_norm_res(B_, src_tiles, nt, gname, resid, dst):
                msb = B_.msb
                for half, p in enumerate(src_tiles):
                    act(lambda e, p=p, half=half: e.copy(out=msb[0:nt, half * 512:(half + 1) * 512], in_=p[0:nt, :]), r=[p], w=[msb])
                rms_stats("", msb[0:nt, :], nt, B_.ss, B_.rstd, junk, [msb], D)
                pool(lambda e: e.tensor_tensor(out=msb[0:nt, :], in0=msb[0:nt, :], in1=gpost[gname][0:nt, :], op=ALU.mult),
                     r=[msb, gpost[gname]], w=[msb])
                dve(lambda e: e.scalar_tensor_tensor(out=dst[0:nt, :], in0=msb[0:nt, :], scalar=B_.rstd[0:nt, 0:1], in1=resid[0:nt, :],
                                                     op0=ALU.mult, op1=ALU.add), r=[msb, B_.rstd, resid], w=[dst])

            def cross_attend(B_, KT_, V_, c0, nt):
                qmT, Pmm, omsb, dens, onb, onT = B_.qmT, B_.Pmm, B_.msb, B_.dens, B_.onb, B_.onT
                for hp in range(2):
                    p = getF()
                    for hh in range(2):
                        h = hp * 2 + hh
                        for mb in range(2):
                            col = (hh * 2 + mb) * nt
                            for dc in range(2):
                                mm(p[:, col:col + nt], KT_[:, h * 2 + dc, mb * 128:(mb + 1) * 128], qmT[:, h * 2 + dc, c0:c0 + nt], [KT_, qmT], [p],
                                   start=(dc == 0), stop=(dc == 1))
                    act(lambda e, p=p, hp=hp: e.activation(out=Pmm[:, hp * 4:hp * 4 + 4, 0:nt], in_=p[:, 0:4 * nt].rearrange("p (a t) -> p a t", a=4),
                                                           func=AF.Exp, scale=SC_M), r=[p], w=[Pmm])
                yield
                pd = getS()
                for hp in range(2):
                    p = getF()
                    for hh in range(2):
                        h = hp * 2 + hh
                        for mb in range(2):
                            mm(p[0:nt, hh * 256:(hh + 1) * 256], Pmm[:, h * 2 + mb, 0:nt], V_[:, mb, h * 256:(h + 1) * 256], [Pmm, V_], [p],
                               start=(mb == 0), stop=(mb == 1))
                    for hh in range(2):
                        h = hp * 2 + hh
                        for mb in range(2):
                            mm(pd[0:nt, h:h + 1], Pmm[:, h * 2 + mb, 0:nt], ones_b[:, 0:1], [Pmm, ones_b], [pd], start=(mb == 0), stop=(mb == 1))
                    act(lambda e, p=p, hp=hp: e.copy(out=omsb[0:nt, hp * 512:(hp + 1) * 512], in_=p[0:nt, :]), r=[p], w=[omsb])
                dve(lambda e, pd=pd: e.reciprocal(out=dens[0:nt, :], in_=pd[0:nt, 0:4]), r=[pd], w=[dens])
                yield
                dve(lambda e: e.tensor_tensor(out=onb[0:nt, :].rearrange("p (h e) -> p h e", h=4), in0=omsb[0:nt, :].rearrange("p (h e) -> p h e", h=4),
                                              in1=dens[0:nt, :].unsqueeze(2).broadcast_to([nt, 4, 256]), op=ALU.mult), r=[omsb, dens], w=[onb])
                yield
                pt = getB()
                for c in range(8):
                    pe(lambda e, c=c, pt=pt: e.transpose(out=pt[:, c * nt:(c + 1) * nt], in_=onb[0:nt, c * 128:(c + 1) * 128],
                                                          identity=ident_b[0:nt, 0:nt]), r=[onb, ident_b], w=[pt])
                act(lambda e, pt=pt: e.copy(out=onT[:, :, c0:c0 + nt], in_=pt[:, 0:8 * nt].rearrange("p (c t) -> p c t", c=8)), r=[pt], w=[onT])

            def tile4(ti, B_, smp_bufs=None):
                smp = ti == NOWN // 128
                nt = NS if smp else 128
                t0 = ti * 128
                hcol = SOFF if smp else NPRE + t0
                acc, num, rden, comb, combT, sga, part, mixT = B_.acc, B_.num, B_.rden, B_.comb, B_.combT, B_.sga, B_.part, B_.mixT
                xt, x1, h2T, qmT, onT = B_.xt, B_.x1, B_.h2T, B_.qmT, B_.onT
                dma(lambda e: e.dma_start(out=acc[0:nt], in_=accd.rearrange("g t f -> t g f")[t0:t0 + nt]), w=[acc])
                dma(lambda e: e.dma_start(out=sga[:, :, 0:nt], in_=sgav[:, :, t0:t0 + nt]), w=[sga])
                dma(lambda e: e.dma_start(out=part[:, :, 0:nt], in_=mixpv[:, :, t0:t0 + nt]), w=[part])
                if smp:
                    dma(lambda e: e.dma_start(out=xt[0:NS, :], in_=xs), w=[xt])
                else:
                    dma(lambda e: e.dma_start(out=xt[:], in_=xp[NPRE + t0:NPRE + t0 + 128, :]), w=[xt])
                yield
                dve(lambda e: e.tensor_tensor(out=num[0:nt].rearrange("p h c -> p (h c)"), in0=acc[0:nt, 0, :], in1=acc[0:nt, 1, :], op=ALU.add),
                    r=[acc], w=[num])
                dve(lambda e: e.tensor_tensor(out=num[0:nt].rearrange("p h c -> p (h c)"), in0=num[0:nt].rearrange("p h c -> p (h c)"),
                                              in1=acc[0:nt, 2, :], op=ALU.add), r=[acc, num], w=[num])
                dve(lambda e: e.reciprocal(out=rden[0:nt, :], in_=num[0:nt, :, 128]), r=[num], w=[rden])
                dve(lambda e: e.tensor_tensor(out=comb[0:nt, :].rearrange("p (h d) -> p h d", h=4), in0=num[0:nt, :, 0:128],
                                              in1=rden[0:nt, :].unsqueeze(2).broadcast_to([nt, 4, 128]), op=ALU.mult), r=[num, rden], w=[comb])
                yield
                pt = getB()
                for c in range(4):
                    pe(lambda e, c=c, pt=pt: e.transpose(out=pt[:, c * nt:(c + 1) * nt], in_=comb[0:nt, c * 128:(c + 1) * 128],
                                                          identity=ident_b[0:nt, 0:nt]), r=[comb, ident_b], w=[pt])
                act(lambda e, pt=pt: e.copy(out=combT[:, :, 0:nt], in_=pt[:, 0:4 * nt].rearrange("p (c t) -> p c t", c=4)), r=[pt], w=[combT])
                yield
                for q4 in range(2):
                    p = getF()
                    for cc in range(4):
                        c = q4 * 4 + cc
                        for k in range(4):
                            mm(p[:, cc * nt:(cc + 1) * nt], Wpa[:, k, c * 128:(c + 1) * 128], combT[:, k, 0:nt], [Wpa, combT], [p], start=(k == 0), stop=(k == 3))
                    dve(lambda e, p=p, q4=q4: e.tensor_tensor(out=mixT[:, q4 * 4:(q4 + 1) * 4, 0:nt], in0=p[:, 0:4 * nt].rearrange("p (c t) -> p c t", c=4),
                                                              in1=sga[:, q4 * 4:(q4 + 1) * 4, 0:nt], op=ALU.mult), r=[p, sga], w=[mixT])
                pool(lambda e: e.tensor_tensor(out=mixT[:, :, 0:nt], in0=mixT[:, :, 0:nt], in1=part[:, :, 0:nt], op=ALU.add), r=[mixT, part], w=[mixT])
                yield
                ph = []
                for half in range(2):
                    p = getF()
                    for k in range(8):
                        mm(p[0:nt, :], mixT[:, k, 0:nt], Wout[:, k, half * 512:(half + 1) * 512], [mixT, Wout], [p], start=(k == 0), stop=(k == 7))
                    ph.append(p)
                post_norm_res(B_, ph, nt, "norm_mix_post", xt, x1)
                yield
                norm_part1(x1, nt, B_.xn, B_.ss, B_.rstd, junk)
                yield
                yield
                norm_part2(nt, "norm_mem_pre", B_.xn, h2T, 0)
                yield
                for q4 in range(2):
                    p = getF()
                    for cc in range(4):
                        c = q4 * 4 + cc
                        for k in range(8):
                            mm(p[:, cc * nt:(cc + 1) * nt], Wmq[:, k, c * 128:(c + 1) * 128], h2T[:, k, 0:nt], [Wmq, h2T], [p], start=(k == 0), stop=(k == 7))
                    act(lambda e, p=p, q4=q4: e.copy(out=qmT[:, q4 * 4:(q4 + 1) * 4, 0:nt], in_=p[:, 0:4 * nt].rearrange("p (c t) -> p c t", c=4)),
                        r=[p], w=[qmT])
                yield
                if not smp:
                    for _ in cross_attend(B_, KmT, Vm, 0, 128):
                        yield
                else:
                    kms, vms, KmTs, Vms = smp_bufs
                    for b in range(4):
                        for mb in range(2):
                            dma(lambda e, b=b, mb=mb: e.dma_start(out=kms[:], in_=cmem[b, mb * 128:(mb + 1) * 128, 0, :]), w=[kms])
                            dma(lambda e, b=b, mb=mb: e.dma_start(out=vms[:], in_=cmem[b, mb * 128:(mb + 1) * 128, 1, :]), w=[vms])
                            for q4 in range(2):
                                p = getF()
                                for cc in range(4):
                                    c = q4 * 4 + cc
                                    pe(lambda e, p=p, c=c, cc=cc: e.transpose(out=p[:, cc * 128:(cc + 1) * 128], in_=kms[:, c * 128:(c + 1) * 128],
                                                                               identity=ident_f[:]), r=[kms, ident_f], w=[p])
                                act(lambda e, p=p, q4=q4, mb=mb: e.copy(out=KmTs[:, q4 * 4:(q4 + 1) * 4, mb * 128:(mb + 1) * 128],
                                                                        in_=p[:, :].rearrange("p (c m) -> p c m", c=4)), r=[p], w=[KmTs])
                            dve(lambda e, mb=mb: e.tensor_copy(out=Vms[:, mb, :], in_=vms[:]), r=[vms], w=[Vms])
                        for _ in cross_attend(B_, KmTs, Vms, b * 8, 8):
                            pass
                yield
                ph = []
                for half in range(2):
                    p = getF()
                    for k in range(8):
                        mm(p[0:nt, :], onT[:, k, 0:nt], Wmo[:, k, half * 512:(half + 1) * 512], [onT, Wmo], [p], start=(k == 0), stop=(k == 7))
                    ph.append(p)
                post_norm_res(B_, ph, nt, "norm_mem_post", x1, B_.msb)
                dma(lambda e: e.dma_start(out=x2d[t0:t0 + nt, :], in_=B_.msb[0:nt, :]), r=[B_.msb])

            run_pipeline([tile4(ti, sets[ti % 4]) for ti in range(NOWN // 128)], 4, 4)
            P.barrier()
            S3rd.close()
            with contextlib.ExitStack() as S3:
                kms = sets[1].msb
                vms = sets[1].x1
                KmTs = sb(S3, "c_KmTs", [128, 8, 256], BF16)
                Vms = sb(S3, "c_Vms", [128, 2, 1024], BF16)
                for _ in tile4(NOWN // 128, sets[0], (kms, vms, KmTs, Vms)):
                    pass
                P.barrier()
        if STOP == 4:
            GE.close()
            P.emit()
            return nc
        with contextlib.ExitStack() as S:
            Wg_ = sb(S, "f_Wg", [128, 8, DFF], BF16)
            Wu_ = sb(S, "f_Wu", [128, 8, DFF], BF16)
            Wd_ = sb(S, "f_Wd", [128, 22, D], BF16)
            WgB = [Buf("WgB%d" % i) for i in range(4)]
            WuB = [Buf("WuB%d" % i) for i in range(4)]
            WdB = [Buf("WdB%d" % i) for i in range(2)]
            for q in range(2):
                c0, c1 = q * 1408, (q + 1) * 1408
                dma(lambda e, c0=c0, c1=c1: e.dma_start(out=Wg_[:, :, c0:c1], in_=wview("w_ffn_gate")[:, :, c0:c1]), w=[WgB[2 * q], WgB[2 * q + 1]], q="pool")
                dma(lambda e, c0=c0, c1=c1: e.dma_start(out=Wu_[:, :, c0:c1], in_=wview("w_ffn_up")[:, :, c0:c1]), w=[WuB[2 * q], WuB[2 * q + 1]], q="pool")
            for q in range(2):
                dma(lambda e, q=q: e.dma_start(out=Wd_[:, q * 11:(q + 1) * 11, :], in_=wview("w_ffn_down")[:, q * 11:(q + 1) * 11, :]), w=[WdB[q]], q="pool")
            gpo = sb(S, "f_gpost", [128, D], F32)
            dma(lambda e: e.dma_start(out=gpo[:], in_=nrm["norm_ffn_post"][0:1, :].broadcast_to([128, D])), w=[gpo])
            xt = [sb(S, "f_x%d" % i, [128, D], F32) for i in range(2)]
            xns = [sb(S, "f_xn%d" % i, [128, D], BF16) for i in range(4)]
            junk = sb(S, "f_junk", [128, D], BF16)
            sss = [sb(S, "f_ss%d" % i, [128, 1], F32) for i in range(4)]
            rstds = [sb(S, "f_rstd%d" % i, [128, 1], F32) for i in range(4)]
            ss = sss[0]
            rstd = rstds[0]
            h3Ts = [sb(S, "f_h3T%d" % i, [128, 8, 512], BF16) for i in range(2)]
            actT = sb(S, "f_actT", [128, 22, 512], BF16)
            sgt = [sb(S, "f_sg%d" % i, [128, 512], F32) for i in range(2)]
            fsb = sb(S, "f_fsb", [128, D], F32)
            yt = sb(S, "f_y", [128, D], F32)
            xi = [0]

            def ffn_group(grp):
                smp = grp == NOWN // 512
                nsub = 1 if smp else 4
                ntok = NS if smp else 512
                h3T = h3Ts[grp % 2]
                for s_ in range(nsub):
                    nt = NS if smp else 128
                    t0 = grp * 512 + s_ * 128
                    x_t = xt[xi[0] % 2]
                    xi[0] += 1
                    dma(lambda e, x_t=x_t, t0=t0, nt=nt: e.dma_start(out=x_t[0:nt, :], in_=x2d[t0:t0 + nt, :]), w=[x_t])
                    norm_part1(x_t, nt, xns[s_], sss[s_], rstds[s_], junk)
                yield
                for s_ in range(nsub):
                    nt = NS if smp else 128
                    norm_part2(nt, "norm_ffn_pre", xns[s_], h3T, s_ * 128)
                yield
                for c in range(22):
                    if c == 11:
                        yield
                    pg = getF()
                    pu = getF()
                    qb = (c * 128) // 704
                    qe = (c * 128 + 127) // 704
                    for k in range(8):
                        mm(pg[:, 0:ntok], Wg_[:, k, c * 128:(c + 1) * 128], h3T[:, k, 0:ntok], [WgB[qb], WgB[qe], h3T], [pg], start=(k == 0), stop=(k == 7))
                    for k in range(8):
                        mm(pu[:, 0:ntok], Wu_[:, k, c * 128:(c + 1) * 128], h3T[:, k, 0:ntok], [WuB[qb], WuB[qe], h3T], [pu], start=(k == 0), stop=(k == 7))
                    sg_ = sgt[c % 2]
                    act(lambda e, pg=pg, sg_=sg_: e.activation(out=sg_[:, 0:ntok], in_=pg[:, 0:ntok], func=AF.Silu), r=[pg], w=[sg_])
                    dve(lambda e, pu=pu, sg_=sg_, c=c: e.tensor_tensor(out=actT[:, c, 0:ntok], in0=pu[:, 0:ntok], in1=sg_[:, 0:ntok], op=ALU.mult),
                        r=[pu, sg_], w=[actT])
                yield
                for s_ in range(nsub):
                    nt = NS if smp else 128
                    t0 = grp * 512 + s_ * 128
                    x_t = xt[xi[0] % 2]
                    xi[0] += 1
                    dma(lambda e, x_t=x_t, t0=t0, nt=nt: e.dma_start(out=x_t[0:nt, :], in_=x2d[t0:t0 + nt, :]), w=[x_t])
                    for half in range(2):
                        p = getF()
                        for c in range(22):
                            mm(p[0:nt, :], actT[:, c, s_ * 128:s_ * 128 + nt], Wd_[:, c, half * 512:(half + 1) * 512], [actT, WdB[c // 11]], [p],
                               start=(c == 0), stop=(c == 21))
                        act(lambda e, p=p, half=half, nt=nt: e.copy(out=fsb[0:nt, half * 512:(half + 1) * 512], in_=p[0:nt, :]), r=[p], w=[fsb])
                    rms_stats("", fsb[0:nt, :], nt, ss, rstd, junk, [fsb], D)
                    dve(lambda e, nt=nt: e.scalar_tensor_tensor(out=fsb[0:nt, :], in0=fsb[0:nt, :], scalar=rstd[0:nt, 0:1], in1=gpo[0:nt, :],
                                                                op0=ALU.mult, op1=ALU.mult), r=[fsb, rstd, gpo], w=[fsb])
                    pool(lambda e, nt=nt, x_t=x_t: e.tensor_tensor(out=yt[0:nt, :], in0=fsb[0:nt, :], in1=x_t[0:nt, :], op=ALU.add),
                         r=[fsb, x_t], w=[yt])
                    if smp:
                        dma(lambda e: e.dma_start(out=ys_o, in_=yt[0:NS, :]), r=[yt])
                    else:
                        dma(lambda e, t0=t0: e.dma_start(out=y_o[t0:t0 + 128, :], in_=yt[:]), r=[yt])

            run_pipeline([ffn_group(grp) for grp in range(NOWN // 512 + 1)], 2, 2)
        P.emit()
    return nc


_NC = None


def kernel(**inp):
    global _NC
    f32 = np.float32
    x_prompt = np.asarray(inp["x_prompt"], f32)
    x_sample = np.asarray(inp["x_sample"], f32)
    oh = onehot_tables()
    shared = {"oh": oh, "rel_bias": np.asarray(inp["rel_bias"], f32), "w_f2": np.asarray(inp["w_f2"], f32)[0],
              "b_f2": np.asarray(inp["b_f2"], f32)[0][None, :], "gla_norm": np.asarray(inp["gla_norm"], f32)[0][None, :]}
    for n in NORMS:
        shared[n] = np.asarray(inp[n], f32)[0][None, :]
    for (n, k, m) in WEIGHTS:
        shared[n] = np.ascontiguousarray(np.asarray(inp[n], f32)[0])
    in_maps = []
    for c in range(8):
        b, half = c // 2, c % 2
        if half == 1:
            xp = x_prompt[b]
        else:
            xp = np.concatenate([np.zeros((NPRE, D), f32), x_prompt[b, :NOWN]], axis=0)
        m = dict(shared)
        m["xp"] = np.ascontiguousarray(xp)
        m["xs"] = np.ascontiguousarray(x_sample[4 * c:4 * c + 4].reshape(NS, D))
        m["flag"] = np.full((128, 1), float(half), f32)
        m["cw1"] = np.ascontiguousarray(np.asarray(inp["cache_win1_kv"], f32)[0, 4 * c:4 * c + 4].reshape(4, 128, 2, 512))
        m["cw2"] = np.ascontiguousarray(np.asarray(inp["cache_win2_kv"], f32)[0, 4 * c:4 * c + 4].reshape(4, 512, 2, 512))
        m["cw3"] = np.ascontiguousarray(np.asarray(inp["cache_win3_kv"], f32)[0, 4 * c:4 * c + 4].reshape(4, 2048, 2, 512))
        m["sgla"] = np.ascontiguousarray(np.asarray(inp["state_gla"], f32)[0, 4 * c:4 * c + 4])
        m["cmem"] = np.ascontiguousarray(np.asarray(inp["cache_mem_kv"], f32)[0, 4 * c:4 * c + 4].reshape(4, 256, 2, 1024))
        m["mem"] = np.ascontiguousarray(np.asarray(inp["mem_prompt"], f32)[b])
        in_maps.append(m)
    if _NC is None:
        _NC = build()
    res = run_bass_kernel_spmd(_NC, in_maps, core_ids=list(range(8)))
    R = res.results
    y_prompt = np.zeros((4, 8192, D), f32)
    y_sample = np.zeros((32, 8, D), f32)
    w1p = np.zeros((1, 4, 128, 2, 4, 128), f32)
    w2p = np.zeros((1, 4, 512, 2, 4, 128), f32)
    w3p = np.zeros((1, 4, 2048, 2, 4, 128), f32)
    glap = np.zeros((1, 4, 4, 128, 256), f32)
    memkv = np.zeros((1, 4, 256, 2, 4, 256), f32)
    w1s = np.zeros((1, 32, 8, 2, 4, 128), f32)
    w2s = np.zeros((1, 32, 8, 2, 4, 128), f32)
    w3s = np.zeros((1, 32, 8, 2, 4, 128), f32)
    glas = np.zeros((1, 32, 4, 128, 256), f32)
    for c in range(8):
        b, half = c // 2, c % 2
        r = R[c]
        y_prompt[b, half * NOWN:(half + 1) * NOWN] = r["y"]
        y_sample[4 * c:4 * c + 4] = r["ys"].reshape(4, 8, D)
        if half == 1:
            w1p[0, b] = r["w1p"].reshape(128, 2, 4, 128)
            w2p[0, b] = r["w2p"].reshape(512, 2, 4, 128)
            w3p[0, b] = r["w3p"].reshape(2048, 2, 4, 128)
            glap[0, b] = r["glap"]
        else:
            memkv[0, b] = r["memkv"].reshape(256, 2, 4, 256)
        w1s[0, 4 * c:4 * c + 4] = r["w1s"].reshape(4, 8, 2, 4, 128)
        w2s[0, 4 * c:4 * c + 4] = r["w2s"].reshape(4, 8, 2, 4, 128)
        w3s[0, 4 * c:4 * c + 4] = r["w3s"].reshape(4, 8, 2, 4, 128)
        glas[0, 4 * c:4 * c + 4] = r["glas"]
    return (y_prompt, y_sample, w1p, w2p, w3p, glap, memkv, w1s, w2s, w3s, glas)
```
